# Optimizing a Trainium2 kernel written in Bass

```python
import math
import jax, jax.numpy as jnp
from jax import lax
import numpy as np

D_MODEL = 1024
BATCH = 4
SEQ = 8192
DEPTH = 1

MIX_WIDTH = D_MODEL
POOL_WIDTH = MIX_WIDTH // 2
POOL_WINDOWS = (2, 4, 8, 16)
N_POOL_GROUPS = len(POOL_WINDOWS)
POOL_GROUP_DIM = POOL_WIDTH // N_POOL_GROUPS
ATTN_WIDTH = MIX_WIDTH - POOL_WIDTH
HEAD_DIM = 64
N_HEADS = ATTN_WIDTH // HEAD_DIM
DILATED_PATTERNS = ((128, 1), (512, 4), (2048, 16))
BLOCK = 128
N_BUCKETS = 32
MAX_DISTANCE = 2048
D_FF = 4 * D_MODEL
NORM_EPS = 1e-6
NEG_INF = -1e30

kernel_name = "hybrid_pool_dilated_attn_layer"


def rms_norm(x, g):
    xf = x.astype(jnp.float32)
    y = xf * lax.rsqrt(jnp.mean(xf * xf, axis=-1, keepdims=True) + NORM_EPS)
    return (y * g.astype(jnp.float32)).astype(x.dtype)


def t5_bucket(dist):
    max_exact = N_BUCKETS // 2
    d_f = jnp.maximum(dist, 1).astype(jnp.float32)
    large = max_exact + (jnp.log(d_f / max_exact) / math.log(MAX_DISTANCE / max_exact)
                         * (N_BUCKETS - max_exact)).astype(jnp.int32)
    large = jnp.minimum(large, N_BUCKETS - 1)
    return jnp.where(dist < max_exact, dist, large)


def causal_mean_minus_self(u, window):
    uf = u.astype(jnp.float32)
    s = uf.shape[1]
    c = jnp.pad(jnp.cumsum(uf, axis=1), ((0, 0), (window, 0), (0, 0)))
    window_sum = c[:, window:] - c[:, :s]
    count = jnp.minimum(jnp.arange(1, s + 1), window).astype(jnp.float32)
    return window_sum / count[None, :, None] - uf


def multiscale_pool_mixer(u, pool_w, pool_scale):
    b, s, _ = u.shape
    ug = u.reshape(b, s, N_POOL_GROUPS, POOL_GROUP_DIM)
    pooled = jnp.stack([causal_mean_minus_self(ug[:, :, g], w)
                        for g, w in enumerate(POOL_WINDOWS)], axis=2)
    mixed = jnp.einsum('bsgc,gcd->bsgd', pooled.astype(u.dtype), pool_w)
    mixed = mixed * pool_scale.reshape(N_POOL_GROUPS, POOL_GROUP_DIM)
    return mixed.reshape(b, s, POOL_WIDTH).astype(u.dtype)


def dilated_window_attention(q, k, v, rel_bias, window, dilation):
    b, h, s, d = q.shape
    w = window // dilation
    assert w <= BLOCK
    L = s // dilation
    nb = -(-L // BLOCK)
    Lp = nb * BLOCK

    def to_blocks(t):
        t = t.reshape(b, h, L, dilation, d).transpose(0, 1, 3, 2, 4)
        t = jnp.pad(t, ((0, 0), (0, 0), (0, 0), (0, Lp - L), (0, 0)))
        return t.reshape(b, h, dilation, nb, BLOCK, d)

    def with_prev(t):
        prev = jnp.pad(t[:, :, :, :-1], ((0, 0), (0, 0), (0, 0), (1, 0), (0, 0), (0, 0)))
        return jnp.concatenate([prev, t], axis=-2)

    qb = to_blocks(q)
    kb = with_prev(to_blocks(k))
    vb = with_prev(to_blocks(v))

    qq = jnp.arange(BLOCK)[:, None]
    kk = jnp.arange(2 * BLOCK)[None, :]
    dist = qq + BLOCK - kk
    rel_ok = (dist >= 0) & (dist <= w)
    mask = rel_ok[None] & ((jnp.arange(nb)[:, None, None] > 0) | (kk >= BLOCK)[None])
    bucket = t5_bucket(jnp.clip(dist, 0, w) * dilation)
    bias = rel_bias[bucket].astype(jnp.float32).transpose(2, 0, 1)

    scores = jnp.einsum('bhrnqd,bhrnkd->bhrnqk', qb, kb,
                        preferred_element_type=jnp.float32)
    scores = jnp.where(mask[None, None, None], scores + bias[None, :, None, None], NEG_INF)
    m = jnp.max(scores, axis=-1, keepdims=True)
    p = jnp.exp(scores - m)
    l = jnp.sum(p, axis=-1, keepdims=True)
    acc = jnp.einsum('bhrnqk,bhrnkd->bhrnqd', p.astype(v.dtype), vb,
                     preferred_element_type=jnp.float32)

    def from_blocks(t):
        e = t.shape[-1]
        t = t.reshape(b, h, dilation, Lp, e)[:, :, :, :L]
        return t.transpose(0, 1, 3, 2, 4).reshape(b, h, s, e)

    return from_blocks(acc), from_blocks(m), from_blocks(l)


def dilated_attention_mixer(q, k, v, q_norm_g, k_norm_g, rel_bias):
    b, s, _ = q.shape
    split = lambda t: t.reshape(b, s, N_HEADS, HEAD_DIM).transpose(0, 2, 1, 3)
    qh = rms_norm(split(q), q_norm_g) * (HEAD_DIM ** -0.5)
    kh = rms_norm(split(k), k_norm_g)
    vh = split(v)
    outs = [dilated_window_attention(qh, kh, vh, rel_bias, w, dl)
            for (w, dl) in DILATED_PATTERNS]
    m_all = jnp.max(jnp.stack([o[1] for o in outs]), axis=0)
    num = sum(jnp.exp(mi - m_all) * acc for acc, mi, _ in outs)
    den = sum(jnp.exp(mi - m_all) * li for _, mi, li in outs)
    o = (num / den).astype(q.dtype)
    return o.transpose(0, 2, 1, 3).reshape(b, s, ATTN_WIDTH)


def setup_inputs(seed: int = 0) -> dict:
    key = jax.random.key(seed)
    ks = jax.random.split(key, 12)
    f32 = jnp.float32
    nrm = lambda k, shape, scale: jax.random.normal(k, shape, f32) * scale
    return {
        "x": nrm(ks[0], (BATCH, SEQ, D_MODEL), 1.0),
        "mix_norm_g": 1.0 + nrm(ks[1], (D_MODEL,), 0.02),
        "w_in": nrm(ks[2], (D_MODEL, POOL_WIDTH + 3 * ATTN_WIDTH), D_MODEL ** -0.5),
        "pool_w": nrm(ks[3], (N_POOL_GROUPS, POOL_GROUP_DIM, POOL_GROUP_DIM), POOL_GROUP_DIM ** -0.5),
        "pool_scale": 1.0 + nrm(ks[4], (POOL_WIDTH,), 0.02),
        "q_norm_g": 1.0 + nrm(ks[5], (HEAD_DIM,), 0.02),
        "k_norm_g": 1.0 + nrm(ks[6], (HEAD_DIM,), 0.02),
        "rel_bias": nrm(ks[7], (N_BUCKETS, N_HEADS), 0.1),
        "w_out": nrm(ks[8], (MIX_WIDTH, D_MODEL), MIX_WIDTH ** -0.5),
        "mlp_norm_g": 1.0 + nrm(ks[9], (D_MODEL,), 0.02),
        "w_up": nrm(ks[10], (D_MODEL, D_FF), D_MODEL ** -0.5),
        "w_down": nrm(ks[11], (D_FF, D_MODEL), D_FF ** -0.5),
    }


def reference(x, mix_norm_g, w_in, pool_w, pool_scale, q_norm_g, k_norm_g, rel_bias,
              w_out, mlp_norm_g, w_up, w_down):
    h = x
    for _ in range(DEPTH):
        a = rms_norm(h, mix_norm_g)
        proj = jnp.einsum('bsd,de->bse', a, w_in)
        u_pool = proj[..., :POOL_WIDTH]
        q = proj[..., POOL_WIDTH:POOL_WIDTH + ATTN_WIDTH]
        k = proj[..., POOL_WIDTH + ATTN_WIDTH:POOL_WIDTH + 2 * ATTN_WIDTH]
        v = proj[..., POOL_WIDTH + 2 * ATTN_WIDTH:]
        y_pool = multiscale_pool_mixer(u_pool, pool_w, pool_scale)
        y_attn = dilated_attention_mixer(q, k, v, q_norm_g, k_norm_g, rel_bias)
        mixed = jnp.concatenate([y_pool, y_attn], axis=-1)
        h = h + jnp.einsum('bse,ed->bsd', mixed, w_out)
        c = rms_norm(h, mlp_norm_g)
        ff = jnp.square(jax.nn.relu(jnp.einsum('bsd,df->bsf', c, w_up)))
        h = h + jnp.einsum('bsf,fd->bsd', ff, w_down)
    return h
```

```python
import math
import numpy as np
import concourse.bass as bass
import concourse.mybir as mybir
from concourse.bass_utils import run_bass_kernel_spmd

F32 = mybir.dt.float32
BF16 = mybir.dt.bfloat16
ALU = mybir.AluOpType
AF = mybir.ActivationFunctionType

NCORES = 8
D = 1024
SEQ = 8192
NOWN = 4096
CH = 2048
NCTX = 3 * CH
DFF = 4096
EPS = 1e-6
NEG = -30000.0
DILS = (1, 4, 16)
POOLW = (2, 4, 8, 16)


class Chan:
    def __init__(self, nc, name, step=1):
        self.sem = nc.alloc_semaphore(name=name)
        self.n = 0
        self.step = step

    def next(self):
        self.n += self.step
        return (self, self.n)


class Res:
    def __init__(self):
        self.w = None
        self.r = {}

    def read_waits(self):
        return [self.w]

    def write_waits(self):
        return [self.w] + list(self.r.items())

    def did_read(self, ev):
        ch, v = ev
        if self.r.get(ch, 0) < v:
            self.r[ch] = v

    def did_write(self, ev):
        self.w = ev
        self.r = {}


class Prog:
    ENG = ("pe", "act", "dve", "pool", "sp")

    def __init__(self, nc):
        self.nc = nc
        self.q = {e: [] for e in self.ENG}
        self.waited = {e: {} for e in self.ENG}
        self.chan = {e: Chan(nc, "c_" + e) for e in ("pe", "act", "dve", "pool")}
        self.all_chans = list(self.chan.values())

    def new_chan(self, name, step=16):
        c = Chan(self.nc, name, step)
        self.all_chans.append(c)
        return c

    def op(self, eng, fn, reads=(), writes=(), chan=None, extra=()):
        waits = list(extra)
        for r in reads:
            waits += r.read_waits()
        for w in writes:
            waits += w.write_waits()
        ws = {}
        for ev in waits:
            if ev is None:
                continue
            ch, v = ev
            if self.waited[eng].get(ch, 0) >= v:
                continue
            if ws.get(ch, 0) < v:
                ws[ch] = v
        for ch, v in ws.items():
            self.waited[eng][ch] = v
        ch = chan or self.chan[eng]
        ev = ch.next()
        self.q[eng].append(([(c.sem, v) for c, v in ws.items()], fn, ev))
        for r in reads:
            r.did_read(ev)
        for w in writes:
            w.did_write(ev)
        return ev

    def barrier(self):
        evs = [(c, c.n) for c in self.all_chans if c.n > 0]
        for eng in self.ENG:
            ws = {}
            for ch, v in evs:
                if self.waited[eng].get(ch, 0) >= v:
                    continue
                ws[ch] = v
                self.waited[eng][ch] = v
            if ws:
                self.q[eng].append(([(c.sem, v) for c, v in ws.items()], None, None))

    def emit(self, eng, e):
        for ws, fn, ev in self.q[eng]:
            for sem, v in ws:
                e.wait_ge(sem, v)
            if fn is None:
                continue
            ins = fn(e)
            ins.then_inc(ev[0].sem, ev[0].step)


def grp(fns):
    def f(e):
        last = None
        for fn in fns:
            last = fn(e)
        return last
    return f


def MM(out, lhsT, rhs, start=True, stop=True):
    return lambda e: e.matmul(out, lhsT=lhsT, rhs=rhs, start=start, stop=stop)


def TR(out, in_, ident):
    return lambda e: e.transpose(out=out, in_=in_, identity=ident)


def ACT(out, in_, func, **kw):
    return lambda e: e.activation(out=out, in_=in_, func=func, **kw)


def TT(out, in0, in1, op):
    return lambda e: e.tensor_tensor(out=out, in0=in0, in1=in1, op=op)


def STT(out, in0, scalar, in1, op0, op1):
    return lambda e: e.scalar_tensor_tensor(out=out, in0=in0, scalar=scalar, in1=in1, op0=op0, op1=op1)


def CP(out, in_):
    return lambda e: e.tensor_copy(out=out, in_=in_)


def RCP(out, in_):
    return lambda e: e.reciprocal(out=out, in_=in_)


def DMA(out, in_):
    return lambda e: e.dma_start(out=out, in_=in_)


def _bucket(n):
    if n < 16:
        return n
    d_f = np.float32(max(n, 1))
    val = np.log(d_f / np.float32(16)) / np.float32(math.log(2048 / 16)) * np.float32(16)
    return min(16 + int(np.int32(val)), 31)


def _bias_consts():
    oh = np.zeros((32, 6 * 256), np.float32)
    mk = np.zeros((8, 6 * 256), np.float32)
    for p, dil in enumerate(DILS):
        for which in (0, 1):
            pw = p * 2 + which
            for m in range(256):
                d = m - 127
                valid = (m <= 254) and ((d <= 0) if which == 0 else (d >= 0))
                if valid:
                    dist = (128 + d) if which == 0 else d
                    oh[_bucket(dist * dil), pw * 256 + m] = 1.0
                else:
                    mk[:, pw * 256 + m] = NEG
    return oh, mk


def _host_consts(half):
    oh, mk = _bias_consts()
    ident = np.eye(128, dtype=np.float32)
    jx = ident[::-1].copy()
    swap = np.roll(ident, 64, axis=1).copy()
    bones = np.zeros((128, 128), np.float32)
    bones[:64, :64] = 1.0
    bones[64:, 64:] = 1.0
    invc = np.zeros((128, 64), np.float32)
    for g, w in enumerate(POOLW):
        for i in range(16):
            invc[:, g * 16 + i] = 1.0 / (min(i + 1, w) if half == 0 else w)
    hv = np.full((128, 64), 1.0 if half == 1 else 0.0, np.float32)
    return dict(c_oh=oh, c_mk=mk, c_ident=ident, c_jx=jx, c_swap=swap, c_bones=bones, c_invc=invc, c_hv=hv)


def build():
    nc = bass.Bass("TRN2", target_bir_lowering=False)

    def din(name, shape, dt=F32):
        return nc.dram_tensor(name, list(shape), dt, kind="ExternalInput").ap()

    xc = din("xc", [NCTX, D])
    g_mix = din("mix_norm_g", [D])
    w_in = din("w_in", [D, 2048])
    pool_w = din("pool_w", [4, 128, 128])
    pool_scale = din("pool_scale", [512])
    q_norm_g = din("q_norm_g", [64])
    k_norm_g = din("k_norm_g", [64])
    rel_bias = din("rel_bias", [32, 8])
    w_out = din("w_out", [D, D])
    g_mlp = din("mlp_norm_g", [D])
    w_up = din("w_up", [D, DFF])
    w_down = din("w_down", [DFF, D])
    c_oh = din("c_oh", [32, 1536])
    c_mk = din("c_mk", [8, 1536])
    c_ident = din("c_ident", [128, 128])
    c_jx = din("c_jx", [128, 128])
    c_swap = din("c_swap", [128, 128])
    c_bones = din("c_bones", [128, 128])
    c_invc = din("c_invc", [128, 64])
    c_hv = din("c_hv", [128, 64])
    out = nc.dram_tensor("out", [NOWN, D], F32, kind="ExternalOutput").ap()
    wup_bf = nc.dram_tensor("wup_bf", [D, DFF], BF16).ap()
    wdn_bf = nc.dram_tensor("wdn_bf", [DFF, D], BF16).ap()
    fd = nc.dram_tensor("fd", [8, 1536], F32).ap()

    P = Prog(nc)

    def sb(name, shape, dt):
        return nc.alloc_sbuf_tensor(name, list(shape), dt)

    kv = sb("kv", [128, 2 * 2 * 4 * CH], BF16)
    ypT = sb("ypT", [128, 4 * CH], BF16)
    yaT = sb("yaT", [128, 4 * CH], BF16)
    tabs = sb("tabs", [128, 24 * 256], BF16)
    ident = sb("ident", [128, 128], BF16)
    jx = sb("jx", [128, 128], BF16)
    bones = sb("bones", [128, 128], BF16)
    swapF = sb("swapF", [128, 128], F32)
    carry = sb("carry", [128, 64], F32)
    poolw = sb("poolw", [128, 4 * 128], BF16)
    hvt = sb("hvt", [128, 64], BF16)
    ones64 = sb("ones64", [128, 64], BF16)
    small = sb("small", [128, 96], F32)
    ABF_N = 32768
    AF_N = 8192
    abf = sb("abf", [128, ABF_N], BF16)
    af = sb("af", [128, AF_N], F32)
    ps = nc.alloc_psum_tensor("ps", [128, 8 * 512], F32)

    def bank(i):
        return ps[:, i * 512:(i + 1) * 512]

    def bankbf(i):
        return bank(i).bitcast(BF16)

    R_bank = [Res() for _ in range(8)]

    invc = small[:, 0:64]
    pscale = small[:, 64:68]
    gq_s = small[:, 68:69]
    gk_s = small[:, 69:70]
    eps_t = small[:, 70:71]
    ssx = [small[:, 72 + i:73 + i] for i in range(3)]
    rsx = [small[:, 76 + i:77 + i] for i in range(3)]
    R_ssx = [Res(), Res(), Res()]
    R_rsx = [Res(), Res(), Res()]

    kvv = kv[:].rearrange("p (s k j t) -> p s k j t", s=2, k=2, j=4)

    def Kt(slot, j):
        return kvv[:, slot, 0, j, :]

    def Vt(slot, j):
        return kvv[:, slot, 1, j, :]

    R_kv = [Res(), Res()]
    ypTv = ypT[:].rearrange("p (j t) -> p j t", j=4)
    yaTv = yaT[:].rearrange("p (j t) -> p j t", j=4)
    R_yp = Res()
    R_ya = Res()
    R_const = Res()
    R_tabs = Res()
    poolwv = poolw[:].rearrange("p (g d) -> p g d", g=4)

    qT = abf[:, 0:8192].rearrange("p (j t) -> p j t", j=4)
    R_q = Res()
    R_qt = [Res() for _ in range(4)]
    aT = abf[:, 8192:16384].rearrange("p (k t) -> p k t", k=8)
    wst = [abf[:, 16384 + i * 4096:16384 + (i + 1) * 4096].rearrange("p (k e) -> p k e", k=8) for i in range(2)]
    a_bf = [abf[:, 24576 + i * 1024:24576 + (i + 1) * 1024] for i in range(2)]
    pooled = [abf[:, 26624 + i * 512:26624 + (i + 1) * 512] for i in range(2)]
    sq_bf = [abf[:, 27648 + i * 512:27648 + (i + 1) * 512] for i in range(2)]
    junkA = abf[:, 27648:28672]
    junkC = abf[:, 31744:32768]
    NVT = 69
    vp = [abf[:, 8192 + i * NVT * 128:8192 + (i + 1) * NVT * 128].rearrange("p (n c) -> p n c", c=128) for i in range(2)]
    PT = [abf[:, 25856 + i * 512:25856 + (i + 1) * 512] for i in range(3)]
    PTr = [abf[:, 27392 + i * 512:27392 + (i + 1) * 512] for i in range(3)]
    wo = abf[:, 0:8192].rearrange("p (k d) -> p k d", k=8)
    c_bf = [abf[:, 8192 + i * 1024:8192 + (i + 1) * 1024] for i in range(2)]
    cT = abf[:, 10240:14336].rearrange("p (k t) -> p k t", k=8)
    wup = [abf[:, 14336 + i * 4096:14336 + (i + 1) * 4096].rearrange("p (k f) -> p k f", k=8) for i in range(2)]
    wdn = [abf[:, 22528 + i * 4096:22528 + (i + 1) * 4096].rearrange("p (k d) -> p k d", k=8) for i in range(2)]
    xs = [af[:, i * 1024:(i + 1) * 1024] for i in range(2)] + [abf[:, 28672:30720].bitcast(F32)]
    ubuf = [af[:, 2048 + g * 528:2048 + (g + 1) * 528] for g in range(4)]
    tA = af[:, 4160:4688]
    tB = af[:, 4688:5216]
    t16 = af[:, 5216:5232]
    tC = af[:, 6272:6800]
    gmix_bc = af[:, 5248:6272]
    gmlp_bc = af[:, 7168:8192]
    R_carry = [Res() for _ in range(4)]
    acc = [af[:, i * 2048:(i + 1) * 2048] for i in range(2)]
    hbuf = [af[:, 2048 + i * 1024:2048 + (i + 1) * 1024] for i in range(4)]
    rl = [af[:, 6144 + i * 512:6144 + (i + 1) * 512] for i in range(2)]
    rsqk = [abf[:, 30720 + i * 1024:30720 + (i + 1) * 1024].bitcast(F32) for i in range(2)]

    class NS:
        pass
    R = NS()

    def reset_arena_res():
        R.xs = [Res(), Res(), Res()]
        R.aT = [Res() for _ in range(8)]
        R.aT2 = [[Res() for _ in range(8)] for _ in range(2)]
        R.wst = [Res(), Res()]
        R.abf = [Res(), Res()]
        R.pooled = [Res(), Res()]
        R.sq = [Res(), Res()]
        R.junk = Res()
        R.ubuf = [Res() for _ in range(4)]
        R.gmix = Res()
        R.tA = Res()
        R.tB = Res()
        R.tC = Res()
        R.tD = Res()
        R.t16 = Res()
        R.rsqk = [Res(), Res()]
        R.vp = [Res(), Res()]
        R.PT = [Res(), Res(), Res()]
        R.PTr = [Res(), Res(), Res()]
        R.vp1 = [Res(), Res()]
        R.acc = [Res(), Res()]
        R.wo = Res()
        R.cbf = [Res(), Res()]
        R.cT = Res()
        R.wup = [Res() for _ in range(4)]
        R.wdn = [Res() for _ in range(4)]
        R.hbuf = [Res() for _ in range(4)]
        R.rl = [Res(), Res()]
        R.ffT = Res()
        for i in range(8):
            R_bank[i] = Res()
    reset_arena_res()

    ch_const = P.new_chan("ch_const")
    ch_const2 = P.new_chan("ch_const2")
    ch_xs = [P.new_chan("ch_xs%d" % i) for i in range(3)]
    ch_wst = [P.new_chan("ch_wst%d" % i) for i in range(2)]
    ch_wup = [P.new_chan("ch_wup%d" % i) for i in range(4)]
    ch_wdn = [P.new_chan("ch_wdn%d" % i) for i in range(4)]
    ch_wo = P.new_chan("ch_wo")
    ch_st = [P.new_chan("ch_st%d" % i) for i in range(4)]
    ch_conv = P.new_chan("ch_conv")
    ch_conv2 = P.new_chan("ch_conv2")
    ch_fd = P.new_chan("ch_fd")
    ch_hk = [P.new_chan("ch_hk%d" % i) for i in range(2)]
    ch_g = P.new_chan("ch_g")
    R_conv = Res()
    R_conv2 = Res()

    cnt = {"xs": 0, "wst": 0, "pb": 0, "ss": 0, "tr": 0, "abf": 0, "pooled": 0, "sq": 0, "rsqk": 0}

    def prologue():
        cl = []
        cl.append(("sp", DMA(swapF[:], c_swap)))
        cl.append(("sp", DMA(gmlp_bc, g_mlp.partition_broadcast(128))))
        cl.append(("sp", DMA(invc, c_invc)))
        pwF = af[:, 5632:6144]
        pscb = af[:, 6144:6656]
        cl.append(("sp", DMA(pwF.rearrange("p (g d) -> p g d", g=4), pool_w.rearrange("g c d -> c g d"))))
        cl.append(("sp", DMA(pscb, pool_scale.partition_broadcast(128))))
        qg = q_norm_g.rearrange("(p o) -> p o", o=1)
        kg = k_norm_g.rearrange("(p o) -> p o", o=1)
        cl.append(("sp", DMA(small[0:64, 68:69], qg)))
        cl.append(("sp", DMA(small[64:128, 68:69], qg)))
        cl.append(("sp", DMA(small[0:64, 69:70], kg)))
        cl.append(("sp", DMA(small[64:128, 69:70], kg)))
        cl.append(("pool", DMA(ident[:], c_ident)))
        cl.append(("pool", DMA(jx[:], c_jx)))
        cl.append(("pool", DMA(bones[:], c_bones)))
        cl.append(("pool", DMA(hvt[:], c_hv)))
        rb = af[0:32, 0:8]
        ohs = af[0:32, 8:8 + 1536]
        mks = af[0:8, 2048:2048 + 1536]
        cl.append(("sp", DMA(rb, rel_bias)))
        cl.append(("sp", DMA(ohs, c_oh)))
        cl.append(("sp", DMA(mks, c_mk)))
        R_c1 = Res()
        R_c2 = Res()
        for eng, fn in cl:
            if eng == "sp":
                R_c1.did_write(P.op(eng, fn, chan=ch_const))
            else:
                R_c2.did_write(P.op(eng, fn, chan=ch_const2))
        R_const.did_write(P.op("dve", lambda e: e.memset(small[:, 80:81], 0.0), reads=[R_c1, R_c2]))
        P.op("pool", lambda e: e.memset(eps_t, EPS), reads=[R_const], writes=[R_const])
        P.op("pool", lambda e: e.memset(ones64[:], 1.0), reads=[R_const], writes=[R_const])
        P.op("pool", lambda e: e.memset(carry[:], 0.0), reads=[R_const], writes=[R_const])
        P.op("dve", lambda e: e.tensor_scalar(out=gq_s, in0=gq_s, scalar1=0.125, scalar2=None, op0=ALU.mult),
             reads=[R_const], writes=[R_const])
        P.op("dve", TT(poolw[:], pwF, pscb, ALU.mult), reads=[R_const], writes=[R_const])
        fs = af[0:8, 4096:4096 + 1536]
        R_fs = Res()
        for i in range(3):
            P.op("pe", MM(ps[0:8, i * 512:(i + 1) * 512], rb, ohs[:, i * 512:(i + 1) * 512]),
                 reads=[R_const], writes=[R_bank[i]])
            P.op("dve", TT(fs[:, i * 512:(i + 1) * 512], ps[0:8, i * 512:(i + 1) * 512],
                           mks[:, i * 512:(i + 1) * 512], ALU.add),
                 reads=[R_bank[i], R_const], writes=[R_fs])
        R_fd = Res()
        P.op("sp", DMA(fd, fs), reads=[R_fs], writes=[R_fd], chan=ch_fd)
        tabv = tabs[:].rearrange("p (n w q) -> p n w q", n=24, w=2)
        hk = [abf[:, 8192 + i * 2048:8192 + (i + 1) * 2048].bitcast(F32).rearrange("p (h q) -> p h q", h=8) for i in range(2)]
        jxF = abf[:, 12288:12544].bitcast(F32)
        R_jx = Res()
        P.op("sp", DMA(jxF, c_jx), reads=[R_fd], writes=[R_jx], chan=ch_g)
        R_hk = [Res(), Res()]
        bi = 3
        for pw in range(6):
            p, which = pw // 2, pw % 2
            src = bass.AP(fd.tensor, pw * 256, [[1, 128], [1536, 8], [1, 128]])
            P.op("sp", DMA(hk[pw % 2], src), reads=[R_fd], writes=[R_hk[pw % 2]], chan=ch_hk[pw % 2])
            for hg in range(2):
                b = bi % 8
                bi += 1
                P.op("pe", MM(bank(b).rearrange("p (h q) -> p h q", h=4), jxF, hk[pw % 2][:, hg * 4:(hg + 1) * 4, :]),
                     reads=[R_hk[pw % 2], R_jx], writes=[R_bank[b]])
                P.op("act", ACT(tabv[:, p * 8 + hg * 4:p * 8 + hg * 4 + 4, which, :],
                                bank(b).rearrange("p (h q) -> p h q", h=4), AF.Exp),
                     reads=[R_bank[b]], writes=[R_tabs])

    conv_list = []
    for i in range(8):
        conv_list.append(DMA(wup_bf[i * 128:(i + 1) * 128, :], w_up[i * 128:(i + 1) * 128, :]))
    for i in range(8):
        conv_list.append(DMA(wdn_bf[i * 512:(i + 1) * 512, :].rearrange("(a p) d -> p a d", p=128),
                             w_down[i * 512:(i + 1) * 512, :].rearrange("(a p) d -> p a d", p=128)))

    def convert_all():
        for k, fn in enumerate(conv_list):
            if k < 8:
                R_conv.did_write(P.op("pool", fn, chan=ch_conv))
            else:
                R_conv2.did_write(P.op("pool", fn, chan=ch_conv2))

    w_in_v = w_in.rearrange("(k p) e -> p k e", p=128)

    def rstd_small(xi, n):
        P.op("act", ACT(rsx[xi], ssx[xi], AF.Ln, scale=1.0 / n, bias=eps_t), reads=[R_ssx[xi], R_const], writes=[R_rsx[xi]])
        P.op("act", ACT(rsx[xi], rsx[xi], AF.Exp, scale=-0.5), reads=[R_rsx[xi]], writes=[R_rsx[xi]])

    def units_of(c, s):
        if c == 0:
            return [2, 3] if s == 0 else [0, 2, 3]
        return [0, 1, 2, 3]

    aTb = [aT, yaT[:].rearrange("p (k t) -> p k t", k=8)]

    def xstage_steps(c, s, buf):
        ctx0 = c * CH + s * 1024
        dst = aTb[buf]
        Rdst = R.aT2[buf]
        state = {"pend": None}

        def mk(tt):
            def step():
                ai = None
                if tt < 8:
                    gt = ctx0 // 128 + tt
                    xi = cnt["xs"] % 3
                    cnt["xs"] += 1
                    ai = cnt["abf"] % 2
                    cnt["abf"] += 1
                    P.op("sp", DMA(xs[xi], xc[gt * 128:(gt + 1) * 128, :]), writes=[R.xs[xi]], chan=ch_xs[xi])
                    P.op("act", ACT(a_bf[ai], xs[xi], AF.Square, accum_out=ssx[xi]), reads=[R.xs[xi]], writes=[R.abf[ai], R_ssx[xi]])
                    rstd_small(xi, D)
                    P.op("dve", STT(a_bf[ai], xs[xi], rsx[xi], gmix_bc, ALU.mult, ALU.mult),
                         reads=[R.xs[xi], R_rsx[xi], R.gmix], writes=[R.abf[ai]])
                if state["pend"] is not None:
                    ptt, pai = state["pend"]
                    tb = 0
                    cnt["tr"] += 1
                    trv = bankbf(tb).rearrange("p (k t) -> p k t", k=8)
                    P.op("pe", grp([TR(trv[:, k, :], a_bf[pai][:, k * 128:(k + 1) * 128], ident[:]) for k in range(8)]),
                         reads=[R.abf[pai], R_const], writes=[R_bank[tb]])
                    if ptt % 2 == 0:
                        P.op("act", ACT(dst[:, :, ptt * 128:(ptt + 1) * 128], trv, AF.Copy), reads=[R_bank[tb]], writes=[Rdst[ptt]])
                    else:
                        P.op("dve", CP(dst[:, :, ptt * 128:(ptt + 1) * 128], trv), reads=[R_bank[tb]], writes=[Rdst[ptt]])
                state["pend"] = (tt, ai) if tt < 8 else None
            return step
        return [mk(tt) for tt in range(9)]

    def phase_A(subs):
        useq = [(c, s, u) for (c, s) in subs for u in units_of(c, s)]
        st = {"next": 0}
        slot_of = {}

        def issue_unit_dma():
            i = st["next"]
            if i >= len(useq):
                return
            st["next"] += 1
            c, s, u = useq[i]
            wi = i % 2
            slot_of[(c, s, u)] = wi
            P.op("pool", DMA(wst[wi], w_in_v[:, :, u * 512:(u + 1) * 512]), writes=[R.wst[wi]], chan=ch_wst[wi])

        P.op("sp", DMA(gmix_bc, g_mix.partition_broadcast(128)), writes=[R.gmix], chan=ch_g)
        issue_unit_dma()
        issue_unit_dma()
        steps0 = xstage_steps(subs[0][0], subs[0][1], 0)
        for stp in steps0[:5]:
            stp()
        rest0 = steps0[5:]
        for idx, (c, s) in enumerate(subs):
            nxt = xstage_steps(subs[idx + 1][0], subs[idx + 1][1], (idx + 1) % 2) if idx + 1 < len(subs) else []
            ngroups = sum(8 for u in units_of(c, s) if u != 0)
            prog = {"g": 0, "done": 0}
            pre = rest0 if idx == 0 else []

            def hook(u, nxt=nxt, ngroups=ngroups, prog=prog, pre=pre):
                if pre:
                    pre.pop(0)()
                    return
                if u == 0:
                    return
                prog["g"] += 1
                want = min(len(nxt), -(-len(nxt) * prog["g"] // ngroups))
                while prog["done"] < want:
                    nxt[prog["done"]]()
                    prog["done"] += 1
            phase_A_sub(c, s, idx % 2, slot_of, issue_unit_dma, hook, first=(idx == 0))
            while pre:
                pre.pop(0)()
            while prog["done"] < len(nxt):
                nxt[prog["done"]]()
                prog["done"] += 1

    def phase_A_sub(c, s, buf, slot_of, issue_unit_dma, hook, first=False):
        slot = c % 2
        halo = (c == 0)
        aTc = aTb[buf]
        RaT = R.aT2[buf]
        deferred = []

        def flush():
            while deferred:
                deferred.pop(0)()

        for u in units_of(c, s):
            wi = slot_of[(c, s, u)]
            sbk_outer = (u == 0) or (first and u == units_of(c, s)[0])
            order = [(et, sbk) for sbk in range(2) for et in range(4)] if sbk_outer else [(et, sbk) for et in range(4) for sbk in range(2)]
            for (et, sbk) in order:
                if True:
                    if halo and u == 0 and sbk == 0:
                        continue
                    ct0 = s * 1024 + sbk * 512
                    pb = 1 + cnt["pb"] % 4
                    cnt["pb"] += 1
                    P.op("pe", grp([MM(bank(pb), wst[wi][:, k, et * 128:(et + 1) * 128],
                                       aTc[:, k, sbk * 512:(sbk + 1) * 512], start=(k == 0), stop=(k == 7))
                                    for k in range(8)]),
                         reads=[R.wst[wi]] + RaT[sbk * 4:(sbk + 1) * 4], writes=[R_bank[pb]])
                    if u == 0:
                        g = et
                        ub = ubuf[g]
                        Rub = R.ubuf[g]
                        P.op("act", ACT(ub[:, 0:16], carry[:, g * 16:(g + 1) * 16], AF.Copy), reads=[R_carry[g]], writes=[Rub])
                        P.op("act", ACT(ub[:, 16:528], bank(pb), AF.Copy), reads=[R_bank[pb]], writes=[Rub])
                        P.op("act", ACT(carry[:, g * 16:(g + 1) * 16], ub[:, 512:528], AF.Copy), reads=[Rub], writes=[R_carry[g]])
                        if halo:
                            hook(u)
                            continue
                        w = POOLW[g]
                        src, Rsrc = ub, Rub
                        tmps = [(tA, R.tA), (tB, R.tB)] if g >= 2 else [(tC, R.tC), (tA, R.tA)]
                        sh = 1
                        for lvl in range(g + 1):
                            dst, Rdst = tmps[lvl % 2]
                            lo = 2 * sh - 1
                            P.op("dve" if g >= 2 else "pool", TT(dst[:, lo:528], src[:, lo:528], src[:, lo - sh:528 - sh], ALU.add),
                                 reads=[Rsrc], writes=[Rdst])
                            src, Rsrc = dst, Rdst
                            sh *= 2
                        pi = cnt["pooled"] % 2
                        cnt["pooled"] += 1
                        P.op("dve", STT(pooled[pi], src[:, 16:528], 1.0 / w, ub[:, 16:528], ALU.mult, ALU.subtract),
                             reads=[Rsrc, Rub], writes=[R.pooled[pi]])
                        if c == 1 and s == 0 and sbk == 0:
                            P.op("dve", TT(t16, src[:, 16:32], invc[:, g * 16:(g + 1) * 16], ALU.mult),
                                 reads=[Rsrc, R_const], writes=[R.t16])
                            P.op("dve", TT(pooled[pi][:, 0:16], t16, ub[:, 16:32], ALU.subtract),
                                 reads=[R.t16, Rub], writes=[R.pooled[pi]])

                        def post_u(g=g, pi=pi, ct0=ct0):
                            P.op("pe", MM(bank(7), poolwv[:, g, :], pooled[pi]), reads=[R.pooled[pi], R_const], writes=[R_bank[7]])
                            P.op("act", ACT(ypTv[:, g, ct0:ct0 + 512], bank(7), AF.Copy),
                                 reads=[R_bank[7]], writes=[R_yp])
                        flush()
                        deferred.append(post_u)
                    elif u in (1, 2):
                        j = et
                        si = cnt["sq"] % 2
                        cnt["sq"] += 1
                        P.op("act", ACT(sq_bf[si], bank(pb), AF.Square), reads=[R_bank[pb]], writes=[R.sq[si]])

                        def post_qk(u=u, j=j, si=si, pb=pb, ct0=ct0):
                            sb_ = 5 + cnt["ss"] % 2
                            cnt["ss"] += 1
                            P.op("pe", MM(bank(sb_), bones[:], sq_bf[si]), reads=[R.sq[si], R_const], writes=[R_bank[sb_]])
                            ri = cnt["rsqk"] % 2
                            cnt["rsqk"] += 1
                            P.op("act", ACT(rsqk[ri], bank(sb_), AF.Ln, scale=1.0 / 64, bias=eps_t),
                                 reads=[R_bank[sb_], R_const], writes=[R.rsqk[ri]])
                            P.op("act", ACT(rsqk[ri], rsqk[ri], AF.Exp, scale=-0.5), reads=[R.rsqk[ri]], writes=[R.rsqk[ri]])
                            if u == 1:
                                P.op("dve", STT(qT[:, j, ct0:ct0 + 512], bank(pb), gq_s, rsqk[ri], ALU.mult, ALU.mult),
                                     reads=[R_bank[pb], R.rsqk[ri], R_const], writes=[R_qt[j]])
                            else:
                                P.op("dve", STT(Kt(slot, j)[:, ct0:ct0 + 512], bank(pb), gk_s, rsqk[ri], ALU.mult, ALU.mult),
                                     reads=[R_bank[pb], R.rsqk[ri], R_const], writes=[R_kv[slot]])
                        flush()
                        deferred.append(post_qk)
                    else:
                        j = et
                        flush()
                        P.op("dve", CP(Vt(slot, j)[:, ct0:ct0 + 512], bank(pb)), reads=[R_bank[pb]], writes=[R_kv[slot]])
                    hook(u)
            flush()
            issue_unit_dma()

    wo_src = w_out.rearrange("(k p) d -> p k d", p=128)

    def tile_index(p, r, n):
        nb = 16 // DILS[p]
        base = (0, 17, 37)[p]
        return base + r * (nb + 1) + (n + 1)

    def tokset(p, r, n):
        dil = DILS[p]
        nb = 16 // dil
        if n >= 0:
            return 0, n * 128 * dil + r, dil
        return -1, (nb - 1) * 128 * dil + r, dil

    lnl = [af[:, 4096 + i * 512:4096 + (i + 1) * 512] for i in range(2)]

    def phase_B(c):
        slot = c % 2
        pslot = 1 - slot
        R.lnl = [Res(), Res()]
        for e in range(2):
            oc = 64 * (1 - e)
            P.op("dve", lambda e_, dst=vp[e][:, :, oc:oc + 64]: e_.memset(dst, 1.0), reads=[R_const], writes=[R.vp1[e]])
            if c == 1:
                for (i0, st, n) in ((0, 1, 1), (17, 5, 4), (37, 2, 16)):
                    dst = vp[e][:, i0:i0 + st * (n - 1) + 1:st, oc:oc + 64]
                    srcb = bass.AP(hvt[:].tensor, hvt[:].offset, [list(hvt[:].ap[0]), [0, n], [1, 64]])
                    P.op("dve", CP(dst, srcb), reads=[R_const], writes=[R.vp1[e]])
        cn = {"vt": 0, "s": 0, "o": 0, "p": 0, "r": 0}
        tiles = []
        for p in range(3):
            nb = 16 // DILS[p]
            for r in range(DILS[p]):
                for n in range(-1, nb):
                    tiles.append((p, r, n))
        assert len(tiles) == NVT

        def vprime_group(h, t0, vb=4):
            j, e = h // 2, h % 2
            bp = 64 * e
            idb = ident[bp:bp + 64, bp:bp + 64]
            nt = min(16, NVT - t0)
            cn["vt"] += 1
            fns = []
            for k in range(nt):
                p, r, n = tiles[t0 + k]
                co, st, step = tokset(p, r, n)
                src = Vt(slot if co == 0 else pslot, j)[bp:bp + 64, st:st + 127 * step + 1:step]
                fns.append(TR(bankbf(vb)[:, k * 64:(k + 1) * 64], src, idb))
            P.op("pe", grp(fns), reads=[R_kv[0], R_kv[1], R_const], writes=[R_bank[vb]])
            P.op("dve", CP(vp[e][:, t0:t0 + nt, bp:bp + 64],
                           bankbf(vb)[:, 0:nt * 64].rearrange("p (n d) -> p n d", d=64)),
                 reads=[R_bank[vb]], writes=[R.vp[e]])

        def normalise(h, s4s=(0, 1, 2, 3)):
            j, e = h // 2, h % 2
            accv = acc[h % 2]
            Racc = R.acc[h % 2]
            nrow = 64 * e
            for s4 in s4s:
                rb_ = 6 + cn["r"] % 2
                li = cn["r"] % 2
                cn["r"] += 1
                P.op("pe", MM(bank(rb_), swapF[:], accv[:, s4 * 512:(s4 + 1) * 512]),
                     reads=[Racc, R_const], writes=[R_bank[rb_]])
                lv = lnl[li][nrow:nrow + 64, :]
                P.op("act", ACT(lv, bank(rb_)[nrow:nrow + 64, :], AF.Ln), reads=[R_bank[rb_]], writes=[R.lnl[li]])
                P.op("act", ACT(lv, lv, AF.Exp, scale=-1.0), reads=[R.lnl[li]], writes=[R.lnl[li]])
                P.op("dve", TT(yaTv[nrow:nrow + 64, j, s4 * 512:(s4 + 1) * 512],
                               accv[nrow:nrow + 64, s4 * 512:(s4 + 1) * 512], lv, ALU.mult),
                     reads=[Racc, R.lnl[li]], writes=[R_ya])

        pairs = []
        for h in range(8):
            for p in range(3):
                if p == 0:
                    blocks = [(0, n) for n in range(16)]
                elif p == 1:
                    blocks = [(r, n) for n in range(4) for r in range(4)]
                else:
                    blocks = [(r, 0) for r in range(16)]
                for g4 in range(4):
                    for pr in range(2):
                        pairs.append((h, p, g4, pr, blocks[g4 * 4 + pr * 2:g4 * 4 + pr * 2 + 2]))

        def scores_prep(item):
            h, p, g4, pr, pair = item
            j, e = h // 2, h % 2
            bp = 64 * e
            sbk = (0, 1, 5)[cn["s"] % 3]
            cn["s"] += 1
            S = bank(sbk)
            tab = tabs[:, (p * 8 + h) * 256:(p * 8 + h + 1) * 256]
            tab_b = bass.AP(tab.tensor, tab.offset, [list(tab.ap[0]), [0, 2], [1, 256]])
            fns = []
            for bi_, (r, n) in enumerate(pair):
                _, qs, step = tokset(p, r, n)
                qv = qT[bp:bp + 64, j, qs:qs + 127 * step + 1:step]
                for wh in range(2):
                    co, ks, _ = tokset(p, r, n - 1 + wh)
                    kvw = Kt(slot if co == 0 else pslot, j)[bp:bp + 64, ks:ks + 127 * step + 1:step]
                    col = bi_ * 256 + wh * 128
                    fns.append(MM(S[:, col:col + 128], kvw, qv, start=True, stop=True))
            pi = cn["p"] % 3
            cn["p"] += 1
            eng = "dve" if cn["p"] % 2 == 0 else "pool"

            def post():
                P.op("act", ACT(PTr[pi], S, AF.Exp), reads=[R_bank[sbk]], writes=[R.PTr[pi]])
                P.op(eng, TT(PT[pi].rearrange("p (a b) -> p a b", a=2), PTr[pi].rearrange("p (a b) -> p a b", a=2), tab_b, ALU.mult),
                     reads=[R.PTr[pi], R_tabs], writes=[R.PT[pi]])
            return (fns, [R_qt[j], R_kv[0], R_kv[1]], [R_bank[sbk]]), post, pi

        def pv_prep(item, pi):
            h, p, g4, pr, pair = item
            e = h % 2
            ai = h % 2
            accv = acc[ai]
            if pr == 0:
                cn["o"] += 1
            ob = 2 + cn["o"] % 2
            fns = []
            for bi_, (r, n) in enumerate(pair):
                pos = pr * 2 + bi_
                for wh in range(2):
                    ti = tile_index(p, r, n - 1 + wh)
                    fns.append(MM(bank(ob)[:, pos * 128:(pos + 1) * 128], vp[e][:, ti, :],
                                  PT[pi][:, bi_ * 256 + wh * 128:bi_ * 256 + wh * 128 + 128],
                                  start=(wh == 0), stop=(wh == 1)))

            def post():
                if pr != 1:
                    return
                if p == 0:
                    P.op("dve", CP(accv[:, g4 * 512:(g4 + 1) * 512], bank(ob)), reads=[R_bank[ob]], writes=[R.acc[ai]])
                else:
                    if p == 1:
                        av = accv[:, g4 * 512:(g4 + 1) * 512].rearrange("p (i r) -> p r i", r=4)
                    else:
                        av = accv.rearrange("p (i r) -> p r i", r=16)[:, g4 * 4:(g4 + 1) * 4, :]
                    P.op("dve", TT(av, av, bank(ob).rearrange("p (r i) -> p r i", r=4), ALU.add),
                         reads=[R_bank[ob], R.acc[ai]], writes=[R.acc[ai]])
            return (fns, [R.vp[e], R.vp1[e], R.PT[pi]], [R_bank[ob]]), post

        for gi_, t0 in enumerate(range(0, NVT, 16)):
            vprime_group(0, t0, vb=(4, 6, 7, 4, 6)[gi_])
        NP = len(pairs)
        LOOK = 2
        pis = {}
        for i in range(NP + LOOK):
            parts, posts = [], []
            if i < NP:
                part, post, pis[i] = scores_prep(pairs[i])
                parts.append(part)
                posts.append(post)
            if i - LOOK >= 0:
                part, post = pv_prep(pairs[i - LOOK], pis.pop(i - LOOK))
                parts.append(part)
                posts.append(post)
            parts.reverse()
            posts.reverse()
            P.op("pe", grp([f for it in parts for f in it[0]]), reads=[r for it in parts for r in it[1]],
                 writes=[w for it in parts for w in it[2]])
            for post in posts:
                post()
            if i < NP:
                h = pairs[i][0]
                k = i % 24
                if h + 1 < 8 and k in (4, 8, 12, 16, 20):
                    vprime_group(h + 1, (k // 4 - 1) * 16)
                if h >= 1 and k in (3, 8, 13, 18):
                    normalise(h - 1, ((k - 3) // 5,))
                if h % 2 == 1 and k == 23:
                    jq = h // 2
                    P.op("pool", DMA(wo[:, 2 * jq:2 * jq + 2, :], wo_src[:, 2 * jq:2 * jq + 2, :]),
                         writes=[R_qt[jq]], chan=ch_wo)
        normalise(7)

    wup_v = wup_bf.rearrange("(k p) f -> p k f", p=128)
    wdn_v = wdn_bf.rearrange("(f p) d -> p f d", p=128)
    out_v = out
    NWB = 4
    wupC = [abf[:, 14336 + i * 1024:14336 + (i + 1) * 1024].rearrange("p (k f) -> p k f", k=8) for i in range(NWB)]
    wdnC = [abf[:, 18432 + i * 1024:18432 + (i + 1) * 1024].rearrange("p (k d) -> p k d", k=2) for i in range(NWB)]
    hbB = [abf[:, 22528 + i * 2048:22528 + (i + 1) * 2048].bitcast(F32) for i in range(4)]
    hb2 = [hbuf, hbB]

    def phase_C(c):
        slot = c % 2
        ffT = kv[:, (1 - slot) * 16384:(2 - slot) * 16384].rearrange("p (f t) -> p f t", f=32)
        R.hb2 = [[Res() for _ in range(4)] for _ in range(2)]
        cc = {"xs": 0, "cbf": 0, "u": 0, "wup": 0, "wdn": 0, "rl": 0}

        def op_part1(blk, tt, b0=0):
            hb, Rhb = hb2[blk % 2][tt], R.hb2[blk % 2][tt]
            t0 = blk * 512 + tt * 128
            crow = c * CH + t0
            xi = cc["xs"] % 2
            cc["xs"] += 1
            P.op("sp", DMA(xs[xi], xc[crow:crow + 128, :]), writes=[R.xs[xi]], chan=ch_xs[xi])
            fns = []
            for k in range(8):
                src = ypTv[:, k, t0:t0 + 128] if k < 4 else yaTv[:, k - 4, t0:t0 + 128]
                for dh in range(2):
                    fns.append(MM(bank(b0 + dh), src, wo[:, k, dh * 512:(dh + 1) * 512], start=(k == 0), stop=(k == 7)))
            P.op("pe", grp(fns), reads=[R.wo, R_yp, R_ya], writes=[R_bank[b0], R_bank[b0 + 1]])
            for dh in range(2):
                P.op("dve", TT(hb[:, dh * 512:(dh + 1) * 512], bank(b0 + dh), xs[xi][:, dh * 512:(dh + 1) * 512], ALU.add),
                     reads=[R_bank[b0 + dh], R.xs[xi]], writes=[Rhb])
            P.op("act", ACT(junkC, hb, AF.Square, accum_out=ssx[xi]), reads=[Rhb], writes=[R.junk, R_ssx[xi]])
            rstd_small(xi, D)
            ci = cc["cbf"] % 2
            cc["cbf"] += 1
            P.op("dve", STT(c_bf[ci], hb, rsx[xi], gmlp_bc, ALU.mult, ALU.mult),
                 reads=[Rhb, R_rsx[xi], R_const], writes=[R.cbf[ci]])
            return ci

        def op_part2(blk, tt, ci, b0=0):
            trv = bankbf(b0).rearrange("p (k t) -> p k t", k=8)
            P.op("pe", grp([TR(trv[:, k, :], c_bf[ci][:, k * 128:(k + 1) * 128], ident[:]) for k in range(8)]),
                 reads=[R.cbf[ci], R_const], writes=[R_bank[b0]])
            P.op("act", ACT(cT[:, :, tt * 128:(tt + 1) * 128], trv, AF.Copy), reads=[R_bank[b0]], writes=[R.cT])

        wq = {"up_issued": 0, "up_used": 0, "dn_issued": 0, "dn_used": 0}

        def issue_wup(n=1):
            for _ in range(n):
                i = wq["up_issued"]
                if i >= 4 * 32 or i - wq["up_used"] >= NWB:
                    return
                wq["up_issued"] += 1
                f = i % 32
                wi = i % NWB
                P.op("sp", DMA(wupC[wi], wup_v[:, :, f * 128:(f + 1) * 128]), reads=[R_conv], writes=[R.wup[wi]], chan=ch_wup[wi])

        def issue_wdn(n=1):
            for _ in range(n):
                i = wq["dn_issued"]
                if i >= 4 * 32 or i - wq["dn_used"] >= NWB:
                    return
                wq["dn_issued"] += 1
                j = i % 32
                dh, g = j // 16, j % 16
                wi = i % NWB
                P.op("sp", DMA(wdnC[wi], wdn_v[:, g * 2:(g + 1) * 2, dh * 512:(dh + 1) * 512]),
                     reads=[R_conv2], writes=[R.wdn[wi]], chan=ch_wdn[wi])

        def up_proj(blk):
            for f in range(32):
                i = wq["up_used"]
                wi = i % NWB
                assert wq["up_issued"] > i
                ub_ = 2 + cc["u"] % 2
                cc["u"] += 1
                P.op("pe", grp([MM(bank(ub_), wupC[wi][:, k, :], cT[:, k, :],
                                   start=(k == 0), stop=(k == 7)) for k in range(8)]),
                     reads=[R.wup[wi], R.cT], writes=[R_bank[ub_]])
                wq["up_used"] += 1
                issue_wup()
                if f >= 24:
                    issue_wdn()
                ri = cc["rl"] % 2
                cc["rl"] += 1
                P.op("act", ACT(rl[ri], bank(ub_), AF.Relu), reads=[R_bank[ub_]], writes=[R.rl[ri]])
                P.op("dve", TT(ffT[:, f, :], rl[ri], rl[ri], ALU.mult), reads=[R.rl[ri]], writes=[R.ffT])

        dn_pend = []

        def flush_dn():
            if dn_pend:
                n = len(dn_pend)
                P.op("pe", grp([f for it in dn_pend for f in it[0]]), reads=[it[1] for it in dn_pend] + [R.ffT],
                     writes=R_bank[4:8])
                del dn_pend[:]
                issue_wdn(n)

        def down_group(blk, dh, g):
            i = wq["dn_used"]
            wi = i % NWB
            assert wq["dn_issued"] > i
            fns = []
            for fi in range(2):
                f = g * 2 + fi
                for tt in range(4):
                    fns.append(MM(bank(4 + tt), ffT[:, f, tt * 128:(tt + 1) * 128], wdnC[wi][:, fi, :],
                                  start=(f == 0), stop=(f == 31)))
            dn_pend.append((fns, R.wdn[wi]))
            wq["dn_used"] += 1
            if len(dn_pend) == 2 or g == 15:
                flush_dn()
            if dh == 1 and g >= 8:
                issue_wup()
            if g == 15:
                for tt in range(4):
                    hb, Rhb = hb2[blk % 2][tt], R.hb2[blk % 2][tt]
                    hv_ = hb[:, dh * 512:(dh + 1) * 512]
                    P.op("dve", TT(hv_, bank(4 + tt), hv_, ALU.add), reads=[R_bank[4 + tt], Rhb], writes=[Rhb])
                    if dh == 1:
                        orow = (c - 1) * CH + blk * 512 + tt * 128
                        P.op("sp", DMA(out_v[orow:orow + 128, :], hb), reads=[Rhb], chan=ch_st[tt])

        issue_wup(NWB)
        pend = None
        for tt in range(4):
            ci = op_part1(0, tt, b0=2 * (tt % 2))
            if pend is not None:
                op_part2(0, *pend)
            pend = (tt, ci, 2 * (tt % 2))
        op_part2(0, *pend)
        for blk in range(4):
            up_proj(blk)
            sched = {}
            if blk + 1 < 4:
                for tt in range(4):
                    sched.setdefault(tt * 6 + 1, []).append(("p1", tt))
                    sched.setdefault(5 + tt * 6, []).append(("p2", tt))
            cis = {}
            gi = 0
            for dh in range(2):
                for g8 in range(16):
                    down_group(blk, dh, g8)
                    if sched.get(gi):
                        flush_dn()
                    for kind, tt in sched.get(gi, []):
                        if kind == "p1":
                            cis[tt] = op_part1(blk + 1, tt)
                        else:
                            op_part2(blk + 1, tt, cis[tt])
                    gi += 1

    prologue()
    P.barrier()
    reset_arena_res()
    for c in (1, 2):
        if c == 1:
            phase_A([(0, 0), (0, 1), (1, 0), (1, 1)])
        else:
            phase_A([(2, 0), (2, 1)])
        P.barrier()
        reset_arena_res()
        if c == 1:
            convert_all()
        phase_B(c)
        P.barrier()
        reset_arena_res()
        phase_C(c)
        P.barrier()
        reset_arena_res()

    with nc.allow_non_contiguous_dma(reason="small constant loads"), nc.Block() as block:
        @block.tensor
        def _(e):
            P.emit("pe", e)

        @block.scalar
        def _(e):
            P.emit("act", e)

        @block.vector
        def _(e):
            P.emit("dve", e)

        @block.gpsimd
        def _(e):
            P.emit("pool", e)

        @block.sync
        def _(e):
            P.emit("sp", e)
    return nc


_NC_CACHE = {}


def kernel(x, mix_norm_g, w_in, pool_w, pool_scale, q_norm_g, k_norm_g, rel_bias,
           w_out, mlp_norm_g, w_up, w_down):
    f = lambda a: np.ascontiguousarray(np.asarray(a), dtype=np.float32)
    x = f(x)
    shared = dict(mix_norm_g=f(mix_norm_g), w_in=f(w_in), pool_w=f(pool_w), pool_scale=f(pool_scale),
                  q_norm_g=f(q_norm_g), k_norm_g=f(k_norm_g), rel_bias=f(rel_bias), w_out=f(w_out),
                  mlp_norm_g=f(mlp_norm_g), w_up=f(w_up), w_down=f(w_down))
    consts = [_host_consts(0), _host_consts(1)]
    in_maps = []
    for i in range(NCORES):
        b, half = i // 2, i % 2
        xcx = np.zeros((NCTX, D), np.float32)
        if half == 0:
            xcx[CH:] = x[b, 0:NOWN]
        else:
            xcx[:] = x[b, NOWN - CH:SEQ]
        m = dict(shared)
        m.update(consts[half])
        m["xc"] = xcx
        in_maps.append(m)
    if "nc" not in _NC_CACHE:
        _NC_CACHE["nc"] = build()
    res = run_bass_kernel_spmd(_NC_CACHE["nc"], in_maps, core_ids=list(range(NCORES)))
    outp = np.empty((4, SEQ, D), np.float32)
    for i in range(NCORES):
        b, half = i // 2, i % 2
        outp[b, half * NOWN:(half + 1) * NOWN] = res.results[i]["out"]
    return outp
```

```python
import math
import numpy as np
import concourse.bass as bass
import concourse.mybir as mybir
from concourse.bass_utils import run_bass_kernel_spmd

F32 = mybir.dt.float32
BF16 = mybir.dt.bfloat16
ALU = mybir.AluOpType
AF = mybir.ActivationFunctionType

NCORES = 8
D = 1024
SEQ = 8192
NOWN = 4096
CH = 2048
NCTX = 3 * CH
DFF = 4096
EPS = 1e-6
NEG = -30000.0
DILS = (1, 4, 16)
POOLW = (2, 4, 8, 16)


class Chan:
    def __init__(self, nc, name, step=1):
        self.sem = nc.alloc_semaphore(name=name)
        self.n = 0
        self.step = step

    def next(self):
        self.n += self.step
        return (self, self.n)


class Res:
    def __init__(self):
        self.w = None
        self.r = {}

    def read_waits(self):
        return [self.w]

    def write_waits(self):
        return [self.w] + list(self.r.items())

    def did_read(self, ev):
        ch, v = ev
        if self.r.get(ch, 0) < v:
            self.r[ch] = v

    def did_write(self, ev):
        self.w = ev
        self.r = {}


class Prog:
    ENG = ("pe", "act", "dve", "pool", "sp")

    def __init__(self, nc):
        self.nc = nc
        self.q = {e: [] for e in self.ENG}
        self.waited = {e: {} for e in self.ENG}
        self.chan = {e: Chan(nc, "c_" + e) for e in ("pe", "act", "dve", "pool")}
        self.all_chans = list(self.chan.values())

    def new_chan(self, name, step=16):
        c = Chan(self.nc, name, step)
        self.all_chans.append(c)
        return c

    def op(self, eng, fn, reads=(), writes=(), chan=None, extra=()):
        waits = list(extra)
        for r in reads:
            waits += r.read_waits()
        for w in writes:
            waits += w.write_waits()
        ws = {}
        for ev in waits:
            if ev is None:
                continue
            ch, v = ev
            if self.waited[eng].get(ch, 0) >= v:
                continue
            if ws.get(ch, 0) < v:
                ws[ch] = v
        for ch, v in ws.items():
            self.waited[eng][ch] = v
        ch = chan or self.chan[eng]
        ev = ch.next()
        self.q[eng].append(([(c.sem, v) for c, v in ws.items()], fn, ev))
        for r in reads:
            r.did_read(ev)
        for w in writes:
            w.did_write(ev)
        return ev

    def barrier(self):
        evs = [(c, c.n) for c in self.all_chans if c.n > 0]
        for eng in self.ENG:
            ws = {}
            for ch, v in evs:
                if self.waited[eng].get(ch, 0) >= v:
                    continue
                ws[ch] = v
                self.waited[eng][ch] = v
            if ws:
                self.q[eng].append(([(c.sem, v) for c, v in ws.items()], None, None))

    def emit(self, eng, e):
        for ws, fn, ev in self.q[eng]:
            for sem, v in ws:
                e.wait_ge(sem, v)
            if fn is None:
                continue
            ins = fn(e)
            ins.then_inc(ev[0].sem, ev[0].step)


def grp(fns):
    def f(e):
        last = None
        for fn in fns:
            last = fn(e)
        return last
    return f


def MM(out, lhsT, rhs, start=True, stop=True):
    return lambda e: e.matmul(out, lhsT=lhsT, rhs=rhs, start=start, stop=stop)


def TR(out, in_, ident):
    return lambda e: e.transpose(out=out, in_=in_, identity=ident)


def ACT(out, in_, func, **kw):
    return lambda e: e.activation(out=out, in_=in_, func=func, **kw)


def TT(out, in0, in1, op):
    return lambda e: e.tensor_tensor(out=out, in0=in0, in1=in1, op=op)


def STT(out, in0, scalar, in1, op0, op1):
    return lambda e: e.scalar_tensor_tensor(out=out, in0=in0, scalar=scalar, in1=in1, op0=op0, op1=op1)


def CP(out, in_):
    return lambda e: e.tensor_copy(out=out, in_=in_)


def RCP(out, in_):
    return lambda e: e.reciprocal(out=out, in_=in_)


def DMA(out, in_):
    return lambda e: e.dma_start(out=out, in_=in_)


def _bucket(n):
    if n < 16:
        return n
    d_f = np.float32(max(n, 1))
    val = np.log(d_f / np.float32(16)) / np.float32(math.log(2048 / 16)) * np.float32(16)
    return min(16 + int(np.int32(val)), 31)


def _bias_consts():
    oh = np.zeros((32, 6 * 256), np.float32)
    mk = np.zeros((8, 6 * 256), np.float32)
    for p, dil in enumerate(DILS):
        for which in (0, 1):
            pw = p * 2 + which
            for m in range(256):
                d = m - 127
                valid = (m <= 254) and ((d <= 0) if which == 0 else (d >= 0))
                if valid:
                    dist = (128 + d) if which == 0 else d
                    oh[_bucket(dist * dil), pw * 256 + m] = 1.0
                else:
                    mk[:, pw * 256 + m] = NEG
    return oh, mk


def _host_consts(half):
    oh, mk = _bias_consts()
    ident = np.eye(128, dtype=np.float32)
    jx = ident[::-1].copy()
    swap = np.roll(ident, 64, axis=1).copy()
    bones = np.zeros((128, 128), np.float32)
    bones[:64, :64] = 1.0
    bones[64:, 64:] = 1.0
    invc = np.zeros((128, 64), np.float32)
    for g, w in enumerate(POOLW):
        for i in range(16):
            invc[:, g * 16 + i] = 1.0 / (min(i + 1, w) if half == 0 else w)
    hv = np.full((128, 64), 1.0 if half == 1 else 0.0, np.float32)
    return dict(c_oh=oh, c_mk=mk, c_ident=ident, c_jx=jx, c_swap=swap, c_bones=bones, c_invc=invc, c_hv=hv)


def build():
    nc = bass.Bass("TRN2", target_bir_lowering=False)

    def din(name, shape, dt=F32):
        return nc.dram_tensor(name, list(shape), dt, kind="ExternalInput").ap()

    xc = din("xc", [NCTX, D])
    g_mix = din("mix_norm_g", [D])
    w_in = din("w_in", [D, 2048])
    pool_w = din("pool_w", [4, 128, 128])
    pool_scale = din("pool_scale", [512])
    q_norm_g = din("q_norm_g", [64])
    k_norm_g = din("k_norm_g", [64])
    rel_bias = din("rel_bias", [32, 8])
    w_out = din("w_out", [D, D])
    g_mlp = din("mlp_norm_g", [D])
    w_up = din("w_up", [D, DFF])
    w_down = din("w_down", [DFF, D])
    c_oh = din("c_oh", [32, 1536])
    c_mk = din("c_mk", [8, 1536])
    c_ident = din("c_ident", [128, 128])
    c_jx = din("c_jx", [128, 128])
    c_swap = din("c_swap", [128, 128])
    c_bones = din("c_bones", [128, 128])
    c_invc = din("c_invc", [128, 64])
    c_hv = din("c_hv", [128, 64])
    out = nc.dram_tensor("out", [NOWN, D], F32, kind="ExternalOutput").ap()
    wup_bf = nc.dram_tensor("wup_bf", [D, DFF], BF16).ap()
    wdn_bf = nc.dram_tensor("wdn_bf", [DFF, D], BF16).ap()
    fd = nc.dram_tensor("fd", [8, 1536], F32).ap()

    P = Prog(nc)

    def sb(name, shape, dt):
        return nc.alloc_sbuf_tensor(name, list(shape), dt)

    kv = sb("kv", [128, 2 * 2 * 4 * CH], BF16)
    ypT = sb("ypT", [128, 4 * CH], BF16)
    yaT = sb("yaT", [128, 4 * CH], BF16)
    tabs = sb("tabs", [128, 24 * 256], BF16)
    ident = sb("ident", [128, 128], BF16)
    jx = sb("jx", [128, 128], BF16)
    bones = sb("bones", [128, 128], BF16)
    swapF = sb("swapF", [128, 128], F32)
    carry = sb("carry", [128, 64], F32)
    poolw = sb("poolw", [128, 4 * 128], BF16)
    hvt = sb("hvt", [128, 64], BF16)
    ones64 = sb("ones64", [128, 64], BF16)
    small = sb("small", [128, 96], F32)
    ABF_N = 32768
    AF_N = 8192
    abf = sb("abf", [128, ABF_N], BF16)
    af = sb("af", [128, AF_N], F32)
    ps = nc.alloc_psum_tensor("ps", [128, 8 * 512], F32)

    def bank(i):
        return ps[:, i * 512:(i + 1) * 512]

    def bankbf(i):
        return bank(i).bitcast(BF16)

    R_bank = [Res() for _ in range(8)]

    invc = small[:, 0:64]
    pscale = small[:, 64:68]
    gq_s = small[:, 68:69]
    gk_s = small[:, 69:70]
    eps_t = small[:, 70:71]
    ssx = [small[:, 72 + i:73 + i] for i in range(3)]
    rsx = [small[:, 76 + i:77 + i] for i in range(3)]
    R_ssx = [Res(), Res(), Res()]
    R_rsx = [Res(), Res(), Res()]

    kvv = kv[:].rearrange("p (s k j t) -> p s k j t", s=2, k=2, j=4)

    def Kt(slot, j):
        return kvv[:, slot, 0, j, :]

    def Vt(slot, j):
        return kvv[:, slot, 1, j, :]

    R_kv = [Res(), Res()]
    ypTv = ypT[:].rearrange("p (j t) -> p j t", j=4)
    yaTv = yaT[:].rearrange("p (j t) -> p j t", j=4)
    R_yp = Res()
    R_ya = Res()
    R_const = Res()
    R_tabs = Res()
    poolwv = poolw[:].rearrange("p (g d) -> p g d", g=4)

    qT = abf[:, 0:8192].rearrange("p (j t) -> p j t", j=4)
    R_q = Res()
    R_qt = [Res() for _ in range(4)]
    aT = abf[:, 8192:16384].rearrange("p (k t) -> p k t", k=8)
    wst = [abf[:, 16384 + i * 4096:16384 + (i + 1) * 4096].rearrange("p (k e) -> p k e", k=8) for i in range(2)]
    a_bf = [abf[:, 24576 + i * 1024:24576 + (i + 1) * 1024] for i in range(2)]
    pooled = [abf[:, 26624 + i * 512:26624 + (i + 1) * 512] for i in range(2)]
    sq_bf = [abf[:, 27648 + i * 512:27648 + (i + 1) * 512] for i in range(2)]
    junkA = abf[:, 27648:28672]
    junkC = abf[:, 31744:32768]
    NVT = 69
    vp = [abf[:, 8192 + i * NVT * 128:8192 + (i + 1) * NVT * 128].rearrange("p (n c) -> p n c", c=128) for i in range(2)]
    PT = [abf[:, 25856 + i * 512:25856 + (i + 1) * 512] for i in range(3)]
    PTr = [abf[:, 27392 + i * 512:27392 + (i + 1) * 512] for i in range(3)]
    wo = abf[:, 0:8192].rearrange("p (k d) -> p k d", k=8)
    c_bf = [abf[:, 8192 + i * 1024:8192 + (i + 1) * 1024] for i in range(2)]
    cT = abf[:, 10240:14336].rearrange("p (k t) -> p k t", k=8)
    wup = [abf[:, 14336 + i * 4096:14336 + (i + 1) * 4096].rearrange("p (k f) -> p k f", k=8) for i in range(2)]
    wdn = [abf[:, 22528 + i * 4096:22528 + (i + 1) * 4096].rearrange("p (k d) -> p k d", k=8) for i in range(2)]
    xs = [af[:, i * 1024:(i + 1) * 1024] for i in range(2)] + [abf[:, 28672:30720].bitcast(F32)]
    ubuf = [af[:, 2048 + g * 528:2048 + (g + 1) * 528] for g in range(4)]
    tA = af[:, 4160:4688]
    tB = af[:, 4688:5216]
    t16 = af[:, 5216:5232]
    tC = af[:, 6272:6800]
    gmix_bc = af[:, 5248:6272]
    gmlp_bc = af[:, 7168:8192]
    R_carry = [Res() for _ in range(4)]
    acc = [af[:, i * 2048:(i + 1) * 2048] for i in range(2)]
    hbuf = [af[:, 2048 + i * 1024:2048 + (i + 1) * 1024] for i in range(4)]
    rl = [af[:, 6144 + i * 512:6144 + (i + 1) * 512] for i in range(2)]
    rsqk = [abf[:, 30720 + i * 1024:30720 + (i + 1) * 1024].bitcast(F32) for i in range(2)]

    class NS:
        pass
    R = NS()

    def reset_arena_res():
        R.xs = [Res(), Res(), Res()]
        R.aT = [Res() for _ in range(8)]
        R.aT2 = [[Res() for _ in range(8)] for _ in range(2)]
        R.wst = [Res(), Res()]
        R.abf = [Res(), Res()]
        R.pooled = [Res(), Res()]
        R.sq = [Res(), Res()]
        R.junk = Res()
        R.ubuf = [Res() for _ in range(4)]
        R.gmix = Res()
        R.tA = Res()
        R.tB = Res()
        R.tC = Res()
        R.tD = Res()
        R.t16 = Res()
        R.rsqk = [Res(), Res()]
        R.vp = [Res(), Res()]
        R.PT = [Res(), Res(), Res()]
        R.PTr = [Res(), Res(), Res()]
        R.vp1 = [Res(), Res()]
        R.acc = [Res(), Res()]
        R.wo = Res()
        R.cbf = [Res(), Res()]
        R.cT = Res()
        R.wup = [Res() for _ in range(4)]
        R.wdn = [Res() for _ in range(4)]
        R.hbuf = [Res() for _ in range(4)]
        R.rl = [Res(), Res()]
        R.ffT = Res()
        for i in range(8):
            R_bank[i] = Res()
    reset_arena_res()

    ch_const = P.new_chan("ch_const")
    ch_const2 = P.new_chan("ch_const2")
    ch_xs = [P.new_chan("ch_xs%d" % i) for i in range(3)]
    ch_wst = [P.new_chan("ch_wst%d" % i) for i in range(2)]
    ch_wup = [P.new_chan("ch_wup%d" % i) for i in range(4)]
    ch_wdn = [P.new_chan("ch_wdn%d" % i) for i in range(4)]
    ch_wo = P.new_chan("ch_wo")
    ch_st = [P.new_chan("ch_st%d" % i) for i in range(4)]
    ch_conv = P.new_chan("ch_conv")
    ch_conv2 = P.new_chan("ch_conv2")
    ch_fd = P.new_chan("ch_fd")
    ch_hk = [P.new_chan("ch_hk%d" % i) for i in range(2)]
    ch_g = P.new_chan("ch_g")
    R_conv = Res()
    R_conv2 = Res()

    cnt = {"xs": 0, "wst": 0, "pb": 0, "ss": 0, "tr": 0, "abf": 0, "pooled": 0, "sq": 0, "rsqk": 0}

    def prologue():
        cl = []
        cl.append(("sp", DMA(swapF[:], c_swap)))
        cl.append(("sp", DMA(gmlp_bc, g_mlp.partition_broadcast(128))))
        cl.append(("sp", DMA(invc, c_invc)))
        pwF = af[:, 5632:6144]
        pscb = af[:, 6144:6656]
        cl.append(("sp", DMA(pwF.rearrange("p (g d) -> p g d", g=4), pool_w.rearrange("g c d -> c g d"))))
        cl.append(("sp", DMA(pscb, pool_scale.partition_broadcast(128))))
        qg = q_norm_g.rearrange("(p o) -> p o", o=1)
        kg = k_norm_g.rearrange("(p o) -> p o", o=1)
        cl.append(("sp", DMA(small[0:64, 68:69], qg)))
        cl.append(("sp", DMA(small[64:128, 68:69], qg)))
        cl.append(("sp", DMA(small[0:64, 69:70], kg)))
        cl.append(("sp", DMA(small[64:128, 69:70], kg)))
        cl.append(("pool", DMA(ident[:], c_ident)))
        cl.append(("pool", DMA(jx[:], c_jx)))
        cl.append(("pool", DMA(bones[:], c_bones)))
        cl.append(("pool", DMA(hvt[:], c_hv)))
        rb = af[0:32, 0:8]
        ohs = af[0:32, 8:8 + 1536]
        mks = af[0:8, 2048:2048 + 1536]
        cl.append(("sp", DMA(rb, rel_bias)))
        cl.append(("sp", DMA(ohs, c_oh)))
        cl.append(("sp", DMA(mks, c_mk)))
        R_c1 = Res()
        R_c2 = Res()
        for eng, fn in cl:
            if eng == "sp":
                R_c1.did_write(P.op(eng, fn, chan=ch_const))
            else:
                R_c2.did_write(P.op(eng, fn, chan=ch_const2))
        R_const.did_write(P.op("dve", lambda e: e.memset(small[:, 80:81], 0.0), reads=[R_c1, R_c2]))
        P.op("pool", lambda e: e.memset(eps_t, EPS), reads=[R_const], writes=[R_const])
        P.op("pool", lambda e: e.memset(ones64[:], 1.0), reads=[R_const], writes=[R_const])
        P.op("pool", lambda e: e.memset(carry[:], 0.0), reads=[R_const], writes=[R_const])
        P.op("dve", lambda e: e.tensor_scalar(out=gq_s, in0=gq_s, scalar1=0.125, scalar2=None, op0=ALU.mult),
             reads=[R_const], writes=[R_const])
        P.op("dve", TT(poolw[:], pwF, pscb, ALU.mult), reads=[R_const], writes=[R_const])
        fs = af[0:8, 4096:4096 + 1536]
        R_fs = Res()
        for i in range(3):
            P.op("pe", MM(ps[0:8, i * 512:(i + 1) * 512], rb, ohs[:, i * 512:(i + 1) * 512]),
                 reads=[R_const], writes=[R_bank[i]])
            P.op("dve", TT(fs[:, i * 512:(i + 1) * 512], ps[0:8, i * 512:(i + 1) * 512],
                           mks[:, i * 512:(i + 1) * 512], ALU.add),
                 reads=[R_bank[i], R_const], writes=[R_fs])
        R_fd = Res()
        P.op("sp", DMA(fd, fs), reads=[R_fs], writes=[R_fd], chan=ch_fd)
        tabv = tabs[:].rearrange("p (n w q) -> p n w q", n=24, w=2)
        hk = [abf[:, 8192 + i * 2048:8192 + (i + 1) * 2048].bitcast(F32).rearrange("p (h q) -> p h q", h=8) for i in range(2)]
        jxF = abf[:, 12288:12544].bitcast(F32)
        R_jx = Res()
        P.op("sp", DMA(jxF, c_jx), reads=[R_fd], writes=[R_jx], chan=ch_g)
        R_hk = [Res(), Res()]
        bi = 3
        for pw in range(6):
            p, which = pw // 2, pw % 2
            src = bass.AP(fd.tensor, pw * 256, [[1, 128], [1536, 8], [1, 128]])
            P.op("sp", DMA(hk[pw % 2], src), reads=[R_fd], writes=[R_hk[pw % 2]], chan=ch_hk[pw % 2])
            for hg in range(2):
                b = bi % 8
                bi += 1
                P.op("pe", MM(bank(b).rearrange("p (h q) -> p h q", h=4), jxF, hk[pw % 2][:, hg * 4:(hg + 1) * 4, :]),
                     reads=[R_hk[pw % 2], R_jx], writes=[R_bank[b]])
                P.op("act", ACT(tabv[:, p * 8 + hg * 4:p * 8 + hg * 4 + 4, which, :],
                                bank(b).rearrange("p (h q) -> p h q", h=4), AF.Exp),
                     reads=[R_bank[b]], writes=[R_tabs])

    conv_list = []
    for i in range(8):
        conv_list.append(DMA(wup_bf[i * 128:(i + 1) * 128, :], w_up[i * 128:(i + 1) * 128, :]))
    for i in range(8):
        conv_list.append(DMA(wdn_bf[i * 512:(i + 1) * 512, :].rearrange("(a p) d -> p a d", p=128),
                             w_down[i * 512:(i + 1) * 512, :].rearrange("(a p) d -> p a d", p=128)))

    def convert_all():
        for k, fn in enumerate(conv_list):
            if k < 8:
                R_conv.did_write(P.op("pool", fn, chan=ch_conv))
            else:
                R_conv2.did_write(P.op("pool", fn, chan=ch_conv2))

    w_in_v = w_in.rearrange("(k p) e -> p k e", p=128)

    def rstd_small(xi, n):
        P.op("act", ACT(rsx[xi], ssx[xi], AF.Ln, scale=1.0 / n, bias=eps_t), reads=[R_ssx[xi], R_const], writes=[R_rsx[xi]])
        P.op("act", ACT(rsx[xi], rsx[xi], AF.Exp, scale=-0.5), reads=[R_rsx[xi]], writes=[R_rsx[xi]])

    def units_of(c, s):
        if c == 0:
            return [2, 3] if s == 0 else [0, 2, 3]
        return [0, 1, 2, 3]

    aTb = [aT, yaT[:].rearrange("p (k t) -> p k t", k=8)]

    def xstage_steps(c, s, buf):
        ctx0 = c * CH + s * 1024
        dst = aTb[buf]
        Rdst = R.aT2[buf]
        state = {"pend": None}

        def mk(tt):
            def step():
                ai = None
                if tt < 8:
                    gt = ctx0 // 128 + tt
                    xi = cnt["xs"] % 3
                    cnt["xs"] += 1
                    ai = cnt["abf"] % 2
                    cnt["abf"] += 1
                    P.op("sp", DMA(xs[xi], xc[gt * 128:(gt + 1) * 128, :]), writes=[R.xs[xi]], chan=ch_xs[xi])
                    P.op("act", ACT(a_bf[ai], xs[xi], AF.Square, accum_out=ssx[xi]), reads=[R.xs[xi]], writes=[R.abf[ai], R_ssx[xi]])
                    rstd_small(xi, D)
                    P.op("dve", STT(a_bf[ai], xs[xi], rsx[xi], gmix_bc, ALU.mult, ALU.mult),
                         reads=[R.xs[xi], R_rsx[xi], R.gmix], writes=[R.abf[ai]])
                if state["pend"] is not None:
                    ptt, pai = state["pend"]
                    tb = 0
                    cnt["tr"] += 1
                    trv = bankbf(tb).rearrange("p (k t) -> p k t", k=8)
                    P.op("pe", grp([TR(trv[:, k, :], a_bf[pai][:, k * 128:(k + 1) * 128], ident[:]) for k in range(8)]),
                         reads=[R.abf[pai], R_const], writes=[R_bank[tb]])
                    P.op("dve", CP(dst[:, :, ptt * 128:(ptt + 1) * 128], trv), reads=[R_bank[tb]], writes=[Rdst[ptt]])
                state["pend"] = (tt, ai) if tt < 8 else None
            return step
        return [mk(tt) for tt in range(9)]

    def phase_A(subs):
        useq = [(c, s, u) for (c, s) in subs for u in units_of(c, s)]
        st = {"next": 0}
        slot_of = {}

        def issue_unit_dma():
            i = st["next"]
            if i >= len(useq):
                return
            st["next"] += 1
            c, s, u = useq[i]
            wi = i % 2
            slot_of[(c, s, u)] = wi
            P.op("pool", DMA(wst[wi], w_in_v[:, :, u * 512:(u + 1) * 512]), writes=[R.wst[wi]], chan=ch_wst[wi])

        P.op("sp", DMA(gmix_bc, g_mix.partition_broadcast(128)), writes=[R.gmix], chan=ch_g)
        issue_unit_dma()
        issue_unit_dma()
        steps0 = xstage_steps(subs[0][0], subs[0][1], 0)
        for stp in steps0[:5]:
            stp()
        rest0 = steps0[5:]
        for idx, (c, s) in enumerate(subs):
            nxt = xstage_steps(subs[idx + 1][0], subs[idx + 1][1], (idx + 1) % 2) if idx + 1 < len(subs) else []
            ngroups = sum(8 for u in units_of(c, s) if u != 0)
            prog = {"g": 0, "done": 0}
            pre = rest0 if idx == 0 else []

            def hook(u, nxt=nxt, ngroups=ngroups, prog=prog, pre=pre):
                if pre:
                    pre.pop(0)()
                    return
                if u == 0:
                    return
                prog["g"] += 1
                want = min(len(nxt), -(-len(nxt) * prog["g"] // ngroups))
                while prog["done"] < want:
                    nxt[prog["done"]]()
                    prog["done"] += 1
            phase_A_sub(c, s, idx % 2, slot_of, issue_unit_dma, hook, first=(idx == 0))
            while pre:
                pre.pop(0)()
            while prog["done"] < len(nxt):
                nxt[prog["done"]]()
                prog["done"] += 1

    def phase_A_sub(c, s, buf, slot_of, issue_unit_dma, hook, first=False):
        slot = c % 2
        halo = (c == 0)
        aTc = aTb[buf]
        RaT = R.aT2[buf]
        deferred = []

        def flush():
            while deferred:
                deferred.pop(0)()

        for u in units_of(c, s):
            wi = slot_of[(c, s, u)]
            sbk_outer = (u == 0) or (first and u == units_of(c, s)[0])
            order = [(et, sbk) for sbk in range(2) for et in range(4)] if sbk_outer else [(et, sbk) for et in range(4) for sbk in range(2)]
            for (et, sbk) in order:
                if True:
                    if halo and u == 0 and sbk == 0:
                        continue
                    ct0 = s * 1024 + sbk * 512
                    pb = 1 + cnt["pb"] % 4
                    cnt["pb"] += 1
                    P.op("pe", grp([MM(bank(pb), wst[wi][:, k, et * 128:(et + 1) * 128],
                                       aTc[:, k, sbk * 512:(sbk + 1) * 512], start=(k == 0), stop=(k == 7))
                                    for k in range(8)]),
                         reads=[R.wst[wi]] + RaT[sbk * 4:(sbk + 1) * 4], writes=[R_bank[pb]])
                    if u == 0:
                        g = et
                        ub = ubuf[g]
                        Rub = R.ubuf[g]
                        P.op("act", ACT(ub[:, 0:16], carry[:, g * 16:(g + 1) * 16], AF.Copy), reads=[R_carry[g]], writes=[Rub])
                        P.op("act", ACT(ub[:, 16:528], bank(pb), AF.Copy), reads=[R_bank[pb]], writes=[Rub])
                        P.op("act", ACT(carry[:, g * 16:(g + 1) * 16], ub[:, 512:528], AF.Copy), reads=[Rub], writes=[R_carry[g]])
                        if halo:
                            hook(u)
                            continue
                        w = POOLW[g]
                        src, Rsrc = ub, Rub
                        tmps = [(tA, R.tA), (tB, R.tB)] if g >= 2 else [(tC, R.tC), (tA, R.tA)]
                        sh = 1
                        for lvl in range(g + 1):
                            dst, Rdst = tmps[lvl % 2]
                            lo = 2 * sh - 1
                            P.op("dve" if g >= 2 else "pool", TT(dst[:, lo:528], src[:, lo:528], src[:, lo - sh:528 - sh], ALU.add),
                                 reads=[Rsrc], writes=[Rdst])
                            src, Rsrc = dst, Rdst
                            sh *= 2
                        pi = cnt["pooled"] % 2
                        cnt["pooled"] += 1
                        P.op("dve", STT(pooled[pi], src[:, 16:528], 1.0 / w, ub[:, 16:528], ALU.mult, ALU.subtract),
                             reads=[Rsrc, Rub], writes=[R.pooled[pi]])
                        if c == 1 and s == 0 and sbk == 0:
                            P.op("dve", TT(t16, src[:, 16:32], invc[:, g * 16:(g + 1) * 16], ALU.mult),
                                 reads=[Rsrc, R_const], writes=[R.t16])
                            P.op("dve", TT(pooled[pi][:, 0:16], t16, ub[:, 16:32], ALU.subtract),
                                 reads=[R.t16, Rub], writes=[R.pooled[pi]])

                        def post_u(g=g, pi=pi, ct0=ct0):
                            P.op("pe", MM(bank(7), poolwv[:, g, :], pooled[pi]), reads=[R.pooled[pi], R_const], writes=[R_bank[7]])
                            P.op("act", ACT(ypTv[:, g, ct0:ct0 + 512], bank(7), AF.Copy),
                                 reads=[R_bank[7]], writes=[R_yp])
                        flush()
                        deferred.append(post_u)
                    elif u in (1, 2):
                        j = et
                        si = cnt["sq"] % 2
                        cnt["sq"] += 1
                        P.op("act", ACT(sq_bf[si], bank(pb), AF.Square), reads=[R_bank[pb]], writes=[R.sq[si]])

                        def post_qk(u=u, j=j, si=si, pb=pb, ct0=ct0):
                            sb_ = 5 + cnt["ss"] % 2
                            cnt["ss"] += 1
                            P.op("pe", MM(bank(sb_), bones[:], sq_bf[si]), reads=[R.sq[si], R_const], writes=[R_bank[sb_]])
                            ri = cnt["rsqk"] % 2
                            cnt["rsqk"] += 1
                            P.op("act", ACT(rsqk[ri], bank(sb_), AF.Ln, scale=1.0 / 64, bias=eps_t),
                                 reads=[R_bank[sb_], R_const], writes=[R.rsqk[ri]])
                            P.op("act", ACT(rsqk[ri], rsqk[ri], AF.Exp, scale=-0.5), reads=[R.rsqk[ri]], writes=[R.rsqk[ri]])
                            if u == 1:
                                P.op("dve", STT(qT[:, j, ct0:ct0 + 512], bank(pb), gq_s, rsqk[ri], ALU.mult, ALU.mult),
                                     reads=[R_bank[pb], R.rsqk[ri], R_const], writes=[R_qt[j]])
                            else:
                                P.op("dve", STT(Kt(slot, j)[:, ct0:ct0 + 512], bank(pb), gk_s, rsqk[ri], ALU.mult, ALU.mult),
                                     reads=[R_bank[pb], R.rsqk[ri], R_const], writes=[R_kv[slot]])
                        flush()
                        deferred.append(post_qk)
                    else:
                        j = et
                        flush()
                        P.op("dve", CP(Vt(slot, j)[:, ct0:ct0 + 512], bank(pb)), reads=[R_bank[pb]], writes=[R_kv[slot]])
                    hook(u)
            flush()
            issue_unit_dma()

    wo_src = w_out.rearrange("(k p) d -> p k d", p=128)

    def tile_index(p, r, n):
        nb = 16 // DILS[p]
        base = (0, 17, 37)[p]
        return base + r * (nb + 1) + (n + 1)

    def tokset(p, r, n):
        dil = DILS[p]
        nb = 16 // dil
        if n >= 0:
            return 0, n * 128 * dil + r, dil
        return -1, (nb - 1) * 128 * dil + r, dil

    lnl = [af[:, 4096 + i * 512:4096 + (i + 1) * 512] for i in range(2)]

    def phase_B(c):
        slot = c % 2
        pslot = 1 - slot
        R.lnl = [Res(), Res()]
        for e in range(2):
            oc = 64 * (1 - e)
            P.op("dve", lambda e_, dst=vp[e][:, :, oc:oc + 64]: e_.memset(dst, 1.0), reads=[R_const], writes=[R.vp1[e]])
            if c == 1:
                for (i0, st, n) in ((0, 1, 1), (17, 5, 4), (37, 2, 16)):
                    dst = vp[e][:, i0:i0 + st * (n - 1) + 1:st, oc:oc + 64]
                    srcb = bass.AP(hvt[:].tensor, hvt[:].offset, [list(hvt[:].ap[0]), [0, n], [1, 64]])
                    P.op("dve", CP(dst, srcb), reads=[R_const], writes=[R.vp1[e]])
        cn = {"vt": 0, "s": 0, "o": 0, "p": 0, "r": 0}
        tiles = []
        for p in range(3):
            nb = 16 // DILS[p]
            for r in range(DILS[p]):
                for n in range(-1, nb):
                    tiles.append((p, r, n))
        assert len(tiles) == NVT

        def vprime_group(h, t0, vb=4):
            j, e = h // 2, h % 2
            bp = 64 * e
            idb = ident[bp:bp + 64, bp:bp + 64]
            nt = min(16, NVT - t0)
            cn["vt"] += 1
            fns = []
            for k in range(nt):
                p, r, n = tiles[t0 + k]
                co, st, step = tokset(p, r, n)
                src = Vt(slot if co == 0 else pslot, j)[bp:bp + 64, st:st + 127 * step + 1:step]
                fns.append(TR(bankbf(vb)[:, k * 64:(k + 1) * 64], src, idb))
            P.op("pe", grp(fns), reads=[R_kv[0], R_kv[1], R_const], writes=[R_bank[vb]])
            P.op("dve", CP(vp[e][:, t0:t0 + nt, bp:bp + 64],
                           bankbf(vb)[:, 0:nt * 64].rearrange("p (n d) -> p n d", d=64)),
                 reads=[R_bank[vb]], writes=[R.vp[e]])

        def normalise(h, s4s=(0, 1, 2, 3)):
            j, e = h // 2, h % 2
            accv = acc[h % 2]
            Racc = R.acc[h % 2]
            nrow = 64 * e
            for s4 in s4s:
                rb_ = 6 + cn["r"] % 2
                li = cn["r"] % 2
                cn["r"] += 1
                P.op("pe", MM(bank(rb_), swapF[:], accv[:, s4 * 512:(s4 + 1) * 512]),
                     reads=[Racc, R_const], writes=[R_bank[rb_]])
                lv = lnl[li][nrow:nrow + 64, :]
                P.op("act", ACT(lv, bank(rb_)[nrow:nrow + 64, :], AF.Ln), reads=[R_bank[rb_]], writes=[R.lnl[li]])
                P.op("act", ACT(lv, lv, AF.Exp, scale=-1.0), reads=[R.lnl[li]], writes=[R.lnl[li]])
                P.op("dve", TT(yaTv[nrow:nrow + 64, j, s4 * 512:(s4 + 1) * 512],
                               accv[nrow:nrow + 64, s4 * 512:(s4 + 1) * 512], lv, ALU.mult),
                     reads=[Racc, R.lnl[li]], writes=[R_ya])

        pairs = []
        for h in range(8):
            for p in range(3):
                if p == 0:
                    blocks = [(0, n) for n in range(16)]
                elif p == 1:
                    blocks = [(r, n) for n in range(4) for r in range(4)]
                else:
                    blocks = [(r, 0) for r in range(16)]
                for g4 in range(4):
                    for pr in range(2):
                        pairs.append((h, p, g4, pr, blocks[g4 * 4 + pr * 2:g4 * 4 + pr * 2 + 2]))

        def scores_prep(item):
            h, p, g4, pr, pair = item
            j, e = h // 2, h % 2
            bp = 64 * e
            sbk = (0, 1, 5)[cn["s"] % 3]
            cn["s"] += 1
            S = bank(sbk)
            tab = tabs[:, (p * 8 + h) * 256:(p * 8 + h + 1) * 256]
            tab_b = bass.AP(tab.tensor, tab.offset, [list(tab.ap[0]), [0, 2], [1, 256]])
            fns = []
            for bi_, (r, n) in enumerate(pair):
                _, qs, step = tokset(p, r, n)
                qv = qT[bp:bp + 64, j, qs:qs + 127 * step + 1:step]
                for wh in range(2):
                    co, ks, _ = tokset(p, r, n - 1 + wh)
                    kvw = Kt(slot if co == 0 else pslot, j)[bp:bp + 64, ks:ks + 127 * step + 1:step]
                    col = bi_ * 256 + wh * 128
                    fns.append(MM(S[:, col:col + 128], kvw, qv, start=True, stop=True))
            pi = cn["p"] % 3
            cn["p"] += 1
            eng = "dve" if cn["p"] % 2 == 0 else "pool"

            def post():
                P.op("act", ACT(PTr[pi], S, AF.Exp), reads=[R_bank[sbk]], writes=[R.PTr[pi]])
                P.op(eng, TT(PT[pi].rearrange("p (a b) -> p a b", a=2), PTr[pi].rearrange("p (a b) -> p a b", a=2), tab_b, ALU.mult),
                     reads=[R.PTr[pi], R_tabs], writes=[R.PT[pi]])
            return (fns, [R_qt[j], R_kv[0], R_kv[1]], [R_bank[sbk]]), post, pi

        def pv_prep(item, pi):
            h, p, g4, pr, pair = item
            e = h % 2
            ai = h % 2
            accv = acc[ai]
            if pr == 0:
                cn["o"] += 1
            ob = 2 + cn["o"] % 2
            fns = []
            for bi_, (r, n) in enumerate(pair):
                pos = pr * 2 + bi_
                for wh in range(2):
                    ti = tile_index(p, r, n - 1 + wh)
                    fns.append(MM(bank(ob)[:, pos * 128:(pos + 1) * 128], vp[e][:, ti, :],
                                  PT[pi][:, bi_ * 256 + wh * 128:bi_ * 256 + wh * 128 + 128],
                                  start=(wh == 0), stop=(wh == 1)))

            def post():
                if pr != 1:
                    return
                if p == 0:
                    P.op("dve", CP(accv[:, g4 * 512:(g4 + 1) * 512], bank(ob)), reads=[R_bank[ob]], writes=[R.acc[ai]])
                else:
                    if p == 1:
                        av = accv[:, g4 * 512:(g4 + 1) * 512].rearrange("p (i r) -> p r i", r=4)
                    else:
                        av = accv.rearrange("p (i r) -> p r i", r=16)[:, g4 * 4:(g4 + 1) * 4, :]
                    P.op("dve", TT(av, av, bank(ob).rearrange("p (r i) -> p r i", r=4), ALU.add),
                         reads=[R_bank[ob], R.acc[ai]], writes=[R.acc[ai]])
            return (fns, [R.vp[e], R.vp1[e], R.PT[pi]], [R_bank[ob]]), post

        for gi_, t0 in enumerate(range(0, NVT, 16)):
            vprime_group(0, t0, vb=(4, 6, 7, 4, 6)[gi_])
        NP = len(pairs)
        LOOK = 2
        pis = {}
        for i in range(NP + LOOK):
            parts, posts = [], []
            if i < NP:
                part, post, pis[i] = scores_prep(pairs[i])
                parts.append(part)
                posts.append(post)
            if i - LOOK >= 0:
                part, post = pv_prep(pairs[i - LOOK], pis.pop(i - LOOK))
                parts.append(part)
                posts.append(post)
            for it, post in zip(parts, posts):
                P.op("pe", grp(it[0]), reads=it[1], writes=it[2])
                post()
            if i < NP:
                h = pairs[i][0]
                k = i % 24
                if h + 1 < 8 and k in (4, 8, 12, 16, 20):
                    vprime_group(h + 1, (k // 4 - 1) * 16)
                if h >= 1 and k in (3, 8, 13, 18):
                    normalise(h - 1, ((k - 3) // 5,))
                if h % 2 == 1 and k == 23:
                    jq = h // 2
                    P.op("pool", DMA(wo[:, 2 * jq:2 * jq + 2, :], wo_src[:, 2 * jq:2 * jq + 2, :]),
                         writes=[R_qt[jq]], chan=ch_wo)
        normalise(7)

    wup_v = wup_bf.rearrange("(k p) f -> p k f", p=128)
    wdn_v = wdn_bf.rearrange("(f p) d -> p f d", p=128)
    out_v = out
    NWB = 4
    wupC = [abf[:, 14336 + i * 1024:14336 + (i + 1) * 1024].rearrange("p (k f) -> p k f", k=8) for i in range(NWB)]
    wdnC = [abf[:, 18432 + i * 1024:18432 + (i + 1) * 1024].rearrange("p (k d) -> p k d", k=2) for i in range(NWB)]
    hbB = [abf[:, 22528 + i * 2048:22528 + (i + 1) * 2048].bitcast(F32) for i in range(4)]
    hb2 = [hbuf, hbB]

    def phase_C(c):
        slot = c % 2
        ffT = kv[:, (1 - slot) * 16384:(2 - slot) * 16384].rearrange("p (f t) -> p f t", f=32)
        R.hb2 = [[Res() for _ in range(4)] for _ in range(2)]
        cc = {"xs": 0, "cbf": 0, "u": 0, "wup": 0, "wdn": 0, "rl": 0}

        def op_part1(blk, tt, b0=0):
            hb, Rhb = hb2[blk % 2][tt], R.hb2[blk % 2][tt]
            t0 = blk * 512 + tt * 128
            crow = c * CH + t0
            xi = cc["xs"] % 2
            cc["xs"] += 1
            P.op("sp", DMA(xs[xi], xc[crow:crow + 128, :]), writes=[R.xs[xi]], chan=ch_xs[xi])
            fns = []
            for k in range(8):
                src = ypTv[:, k, t0:t0 + 128] if k < 4 else yaTv[:, k - 4, t0:t0 + 128]
                for dh in range(2):
                    fns.append(MM(bank(b0 + dh), src, wo[:, k, dh * 512:(dh + 1) * 512], start=(k == 0), stop=(k == 7)))
            P.op("pe", grp(fns), reads=[R.wo, R_yp, R_ya], writes=[R_bank[b0], R_bank[b0 + 1]])
            for dh in range(2):
                P.op("dve", TT(hb[:, dh * 512:(dh + 1) * 512], bank(b0 + dh), xs[xi][:, dh * 512:(dh + 1) * 512], ALU.add),
                     reads=[R_bank[b0 + dh], R.xs[xi]], writes=[Rhb])
            P.op("act", ACT(junkC, hb, AF.Square, accum_out=ssx[xi]), reads=[Rhb], writes=[R.junk, R_ssx[xi]])
            rstd_small(xi, D)
            ci = cc["cbf"] % 2
            cc["cbf"] += 1
            P.op("dve", STT(c_bf[ci], hb, rsx[xi], gmlp_bc, ALU.mult, ALU.mult),
                 reads=[Rhb, R_rsx[xi], R_const], writes=[R.cbf[ci]])
            return ci

        def op_part2(blk, tt, ci, b0=0):
            trv = bankbf(b0).rearrange("p (k t) -> p k t", k=8)
            P.op("pe", grp([TR(trv[:, k, :], c_bf[ci][:, k * 128:(k + 1) * 128], ident[:]) for k in range(8)]),
                 reads=[R.cbf[ci], R_const], writes=[R_bank[b0]])
            P.op("act", ACT(cT[:, :, tt * 128:(tt + 1) * 128], trv, AF.Copy), reads=[R_bank[b0]], writes=[R.cT])

        wq = {"up_issued": 0, "up_used": 0, "dn_issued": 0, "dn_used": 0}

        def issue_wup(n=1):
            for _ in range(n):
                i = wq["up_issued"]
                if i >= 4 * 32 or i - wq["up_used"] >= NWB:
                    return
                wq["up_issued"] += 1
                f = i % 32
                wi = i % NWB
                P.op("sp", DMA(wupC[wi], wup_v[:, :, f * 128:(f + 1) * 128]), reads=[R_conv], writes=[R.wup[wi]], chan=ch_wup[wi])

        def issue_wdn(n=1):
            for _ in range(n):
                i = wq["dn_issued"]
                if i >= 4 * 32 or i - wq["dn_used"] >= NWB:
                    return
                wq["dn_issued"] += 1
                j = i % 32
                dh, g = j // 16, j % 16
                wi = i % NWB
                P.op("sp", DMA(wdnC[wi], wdn_v[:, g * 2:(g + 1) * 2, dh * 512:(dh + 1) * 512]),
                     reads=[R_conv2], writes=[R.wdn[wi]], chan=ch_wdn[wi])

        def up_proj(blk):
            for f in range(32):
                i = wq["up_used"]
                wi = i % NWB
                assert wq["up_issued"] > i
                ub_ = 2 + cc["u"] % 2
                cc["u"] += 1
                P.op("pe", grp([MM(bank(ub_), wupC[wi][:, k, :], cT[:, k, :],
                                   start=(k == 0), stop=(k == 7)) for k in range(8)]),
                     reads=[R.wup[wi], R.cT], writes=[R_bank[ub_]])
                wq["up_used"] += 1
                issue_wup()
                if f >= 24:
                    issue_wdn()
                ri = cc["rl"] % 2
                cc["rl"] += 1
                P.op("act", ACT(rl[ri], bank(ub_), AF.Relu), reads=[R_bank[ub_]], writes=[R.rl[ri]])
                P.op("dve", TT(ffT[:, f, :], rl[ri], rl[ri], ALU.mult), reads=[R.rl[ri]], writes=[R.ffT])

        dn_pend = []

        def flush_dn():
            if dn_pend:
                n = len(dn_pend)
                P.op("pe", grp([f for it in dn_pend for f in it[0]]), reads=[it[1] for it in dn_pend] + [R.ffT],
                     writes=R_bank[4:8])
                del dn_pend[:]
                issue_wdn(n)

        def down_group(blk, dh, g):
            i = wq["dn_used"]
            wi = i % NWB
            assert wq["dn_issued"] > i
            fns = []
            for fi in range(2):
                f = g * 2 + fi
                for tt in range(4):
                    fns.append(MM(bank(4 + tt), ffT[:, f, tt * 128:(tt + 1) * 128], wdnC[wi][:, fi, :],
                                  start=(f == 0), stop=(f == 31)))
            dn_pend.append((fns, R.wdn[wi]))
            wq["dn_used"] += 1
            if len(dn_pend) == 2 or g == 15:
                flush_dn()
            if dh == 1 and g >= 8:
                issue_wup()
            if g == 15:
                for tt in range(4):
                    hb, Rhb = hb2[blk % 2][tt], R.hb2[blk % 2][tt]
                    hv_ = hb[:, dh * 512:(dh + 1) * 512]
                    P.op("dve", TT(hv_, bank(4 + tt), hv_, ALU.add), reads=[R_bank[4 + tt], Rhb], writes=[Rhb])
                    if dh == 1:
                        orow = (c - 1) * CH + blk * 512 + tt * 128
                        P.op("sp", DMA(out_v[orow:orow + 128, :], hb), reads=[Rhb], chan=ch_st[tt])

        issue_wup(NWB)
        pend = None
        for tt in range(4):
            ci = op_part1(0, tt, b0=2 * (tt % 2))
            if pend is not None:
                op_part2(0, *pend)
            pend = (tt, ci, 2 * (tt % 2))
        op_part2(0, *pend)
        for blk in range(4):
            up_proj(blk)
            sched = {}
            if blk + 1 < 4:
                for tt in range(4):
                    sched.setdefault(tt * 6 + 1, []).append(("p1", tt))
                    sched.setdefault(5 + tt * 6, []).append(("p2", tt))
            cis = {}
            gi = 0
            for dh in range(2):
                for g8 in range(16):
                    down_group(blk, dh, g8)
                    if sched.get(gi):
                        flush_dn()
                    for kind, tt in sched.get(gi, []):
                        if kind == "p1":
                            cis[tt] = op_part1(blk + 1, tt)
                        else:
                            op_part2(blk + 1, tt, cis[tt])
                    gi += 1

    prologue()
    P.barrier()
    reset_arena_res()
    for c in (1, 2):
        if c == 1:
            phase_A([(0, 0), (0, 1), (1, 0), (1, 1)])
        else:
            phase_A([(2, 0), (2, 1)])
        P.barrier()
        reset_arena_res()
        if c == 1:
            convert_all()
        phase_B(c)
        P.barrier()
        reset_arena_res()
        phase_C(c)
        P.barrier()
        reset_arena_res()

    with nc.allow_non_contiguous_dma(reason="small constant loads"), nc.Block() as block:
        @block.tensor
        def _(e):
            P.emit("pe", e)

        @block.scalar
        def _(e):
            P.emit("act", e)

        @block.vector
        def _(e):
            P.emit("dve", e)

        @block.gpsimd
        def _(e):
            P.emit("pool", e)

        @block.sync
        def _(e):
            P.emit("sp", e)
    return nc


_NC_CACHE = {}


def kernel(x, mix_norm_g, w_in, pool_w, pool_scale, q_norm_g, k_norm_g, rel_bias,
           w_out, mlp_norm_g, w_up, w_down):
    f = lambda a: np.ascontiguousarray(np.asarray(a), dtype=np.float32)
    x = f(x)
    shared = dict(mix_norm_g=f(mix_norm_g), w_in=f(w_in), pool_w=f(pool_w), pool_scale=f(pool_scale),
                  q_norm_g=f(q_norm_g), k_norm_g=f(k_norm_g), rel_bias=f(rel_bias), w_out=f(w_out),
                  mlp_norm_g=f(mlp_norm_g), w_up=f(w_up), w_down=f(w_down))
    consts = [_host_consts(0), _host_consts(1)]
    in_maps = []
    for i in range(NCORES):
        b, half = i // 2, i % 2
        xcx = np.zeros((NCTX, D), np.float32)
        if half == 0:
            xcx[CH:] = x[b, 0:NOWN]
        else:
            xcx[:] = x[b, NOWN - CH:SEQ]
        m = dict(shared)
        m.update(consts[half])
        m["xc"] = xcx
        in_maps.append(m)
    if "nc" not in _NC_CACHE:
        _NC_CACHE["nc"] = build()
    res = run_bass_kernel_spmd(_NC_CACHE["nc"], in_maps, core_ids=list(range(NCORES)))
    outp = np.empty((4, SEQ, D), np.float32)
    for i in range(NCORES):
        b, half = i // 2, i % 2
        outp[b, half * NOWN:(half + 1) * NOWN] = res.results[i]["out"]
    return outp
```
